# Optimizing a Trainium2 kernel written in Bass

```python
import math
import jax
import jax.numpy as jnp
from jax import lax
import numpy as np


D_MODEL = 1024
BATCH = 8
SEQ = 4096
DEPTH = 2

MEM_LEN = 256
MIX_W = 512
N_BRANCH = 3
S5_GROUP_CH = 16
S5_GROUPS = MIX_W // S5_GROUP_CH
S5_STATE = 64
GLA_HEADS = 4
GLA_DK = 64
GLA_DV = 128
GLA_GATE_RANK = 16
GLA_TAU = 16.0
GLA_CHUNK = 64
ATT_GROUPS = ((128, 1), (512, 4), (2048, 16))
ATT_HEADS_PER_GROUP = 4
N_ATT_HEADS = ATT_HEADS_PER_GROUP * len(ATT_GROUPS)
ATT_HEAD_DIM = MIX_W // ATT_HEADS_PER_GROUP
ATT_BLOCK = 128
ALIBI_MAX_EXP = 8.0
X_HEADS = 4
X_HEAD_DIM = D_MODEL // X_HEADS
D_FF = 4 * D_MODEL
RMS_EPS = 1e-6
PROJ_WIDTHS = (MIX_W,
               GLA_HEADS * GLA_DK,
               GLA_HEADS * GLA_DK,
               GLA_HEADS * GLA_DV,
               GLA_GATE_RANK,
               GLA_HEADS * GLA_DV,
               N_ATT_HEADS * ATT_HEAD_DIM,
               N_ATT_HEADS * ATT_HEAD_DIM,
               N_ATT_HEADS * ATT_HEAD_DIM,
               N_BRANCH * D_MODEL)
D_IN = sum(PROJ_WIDTHS)

kernel_name = 'hybrid_s5_gla_dilated_block'


def rms_norm(x, gain):
    xf = x.astype(jnp.float32)
    y = xf * lax.rsqrt(jnp.mean(xf * xf, axis=-1, keepdims=True) + RMS_EPS)
    return (y * gain.astype(jnp.float32)).astype(x.dtype)


def alibi_slopes(n):
    return 2.0 ** (-ALIBI_MAX_EXP * jnp.arange(1, n + 1, dtype=jnp.float32) / n)


def s5_branch(u, a_re, a_im, log_step, b_re, b_im, c_re, c_im, d_skip, w_glu, b_glu):
    f32 = jnp.float32
    bsz, seq, _ = u.shape
    uf = u.astype(f32)
    lam = lax.complex(a_re.astype(f32), a_im.astype(f32))
    step = jnp.exp(log_step.astype(f32))[:, None]
    lam_bar = jnp.exp(lam * step)
    b_bar = ((lam_bar - 1.0) / lam)[:, :, None] * lax.complex(b_re.astype(f32), b_im.astype(f32))
    ug = uf.reshape(bsz, seq, S5_GROUPS, S5_GROUP_CH).astype(jnp.complex64)
    bu = jnp.einsum('gpc,bsgc->bsgp', b_bar, ug)
    a_seq = jnp.broadcast_to(lam_bar, (1, seq) + lam_bar.shape)

    def combine(left, right):
        a_l, b_l = left
        a_r, b_r = right
        return a_r * a_l, a_r * b_l + b_r

    _, states = lax.associative_scan(combine, (a_seq, bu), axis=1)
    c_mat = lax.complex(c_re.astype(f32), c_im.astype(f32))
    y = jnp.einsum('gcp,bsgp->bsgc', c_mat, states).real.reshape(bsz, seq, MIX_W)
    y = y + d_skip.astype(f32) * uf
    g = jax.nn.gelu(y)
    return (g * jax.nn.sigmoid(g @ w_glu.astype(f32) + b_glu.astype(f32))).astype(u.dtype)


def gla_chunked(q, k, v, log_a):
    bsz, seq, nh, dk = q.shape
    dv = v.shape[-1]
    nc = seq // GLA_CHUNK

    def chunks(t):
        return t.reshape(bsz, nc, GLA_CHUNK, nh, t.shape[-1]).transpose(1, 0, 3, 2, 4)

    causal = jnp.tril(jnp.ones((GLA_CHUNK, GLA_CHUNK), dtype=bool))

    def step(state, inp):
        qc, kc, vc, lac = inp
        b = jnp.cumsum(lac, axis=2)
        inter = jnp.einsum('bhcd,bhde->bhce', qc * jnp.exp(b), state)
        diff = b[:, :, :, None, :] - b[:, :, None, :, :]
        decay = jnp.exp(jnp.where(causal[:, :, None], diff, -jnp.inf))
        scores = jnp.einsum('bhijd,bhjd->bhij', qc[:, :, :, None, :] * decay, kc)
        intra = jnp.einsum('bhij,bhje->bhie', scores, vc)
        b_end = b[:, :, -1:, :]
        new_state = jnp.exp(b_end[:, :, 0, :])[..., None] * state + jnp.einsum('bhjd,bhje->bhde', kc * jnp.exp(b_end - b), vc)
        return new_state, inter + intra

    state0 = jnp.zeros((bsz, nh, dk, dv), jnp.float32)
    _, out = lax.scan(step, state0, (chunks(q), chunks(k), chunks(v), chunks(log_a)))
    return out.transpose(1, 0, 3, 2, 4).reshape(bsz, seq, nh, dv)


def gla_branch(q, k, v, g_lr, r, w_gate, b_gate, g_out):
    f32 = jnp.float32
    bsz, seq, _ = q.shape
    qh = q.astype(f32).reshape(bsz, seq, GLA_HEADS, GLA_DK) * GLA_DK ** -0.5
    kh = k.astype(f32).reshape(bsz, seq, GLA_HEADS, GLA_DK)
    vh = v.astype(f32).reshape(bsz, seq, GLA_HEADS, GLA_DV)
    log_a = jax.nn.log_sigmoid(g_lr.astype(f32) @ w_gate.astype(f32) + b_gate.astype(f32)) / GLA_TAU
    log_a = log_a.reshape(bsz, seq, GLA_HEADS, GLA_DK)
    o = rms_norm(gla_chunked(qh, kh, vh, log_a), g_out)
    return (o.reshape(bsz, seq, MIX_W) * jax.nn.silu(r.astype(f32))).astype(q.dtype)


def dilated_window_attention(q, k, v, slopes, window, dilation):
    bsz, seq, hg, dh = q.shape
    sub_len = seq // dilation
    n_blk = -(-sub_len // ATT_BLOCK)
    pad_len = n_blk * ATT_BLOCK
    steps = window // dilation

    def to_sub(t):
        t = t.reshape(bsz, sub_len, dilation, hg, dh).transpose(0, 2, 1, 3, 4).reshape(bsz * dilation, sub_len, hg, dh)
        return jnp.pad(t, ((0, 0), (0, pad_len - sub_len), (0, 0), (0, 0)))

    def kv_blocks(t):
        tp = jnp.pad(t, ((0, 0), (ATT_BLOCK, 0), (0, 0), (0, 0)))
        prev = tp[:, :pad_len].reshape(-1, n_blk, ATT_BLOCK, hg, dh)
        cur = tp[:, ATT_BLOCK:].reshape(-1, n_blk, ATT_BLOCK, hg, dh)
        return jnp.concatenate([prev, cur], axis=2)

    qb = to_sub(q).reshape(-1, n_blk, ATT_BLOCK, hg, dh)
    kb = kv_blocks(to_sub(k))
    vb = kv_blocks(to_sub(v))
    scores = jnp.einsum('rnqhd,rnkhd->rnhqk', qb, kb) * dh ** -0.5
    blk = jnp.arange(n_blk)[:, None] * ATT_BLOCK
    q_pos = blk + jnp.arange(ATT_BLOCK)[None, :]
    k_pos = blk - ATT_BLOCK + jnp.arange(2 * ATT_BLOCK)[None, :]
    dist = q_pos[:, :, None] - k_pos[:, None, :]
    valid = (dist >= 0) & (dist <= steps) & (k_pos[:, None, :] >= 0)
    bias = -slopes[None, :, None, None] * (dilation * dist).astype(jnp.float32)[:, None]
    logits = jnp.where(valid[:, None], scores + bias, -jnp.inf)
    lse = jax.nn.logsumexp(logits, axis=-1)
    probs = jnp.exp(logits - lse[..., None])
    out = jnp.einsum('rnhqk,rnkhd->rnqhd', probs, vb).reshape(bsz, dilation, pad_len, hg, dh)[:, :, :sub_len]
    out = out.transpose(0, 2, 1, 3, 4).reshape(bsz, seq, hg, dh)
    lse = lse.transpose(0, 1, 3, 2).reshape(bsz, dilation, pad_len, hg)[:, :, :sub_len]
    lse = lse.transpose(0, 2, 1, 3).reshape(bsz, seq, hg)
    return out, lse


def dilated_branch(q, k, v):
    f32 = jnp.float32
    bsz, seq, _ = q.shape
    qh = q.astype(f32).reshape(bsz, seq, N_ATT_HEADS, ATT_HEAD_DIM)
    kh = k.astype(f32).reshape(bsz, seq, N_ATT_HEADS, ATT_HEAD_DIM)
    vh = v.astype(f32).reshape(bsz, seq, N_ATT_HEADS, ATT_HEAD_DIM)
    slopes = alibi_slopes(N_ATT_HEADS)
    outs, lses = [], []
    for g, (window, dilation) in enumerate(ATT_GROUPS):
        hs = slice(g * ATT_HEADS_PER_GROUP, (g + 1) * ATT_HEADS_PER_GROUP)
        o, lse = dilated_window_attention(qh[:, :, hs], kh[:, :, hs], vh[:, :, hs], slopes[hs], window, dilation)
        outs.append(o)
        lses.append(lse)
    outs = jnp.stack(outs, axis=2)
    weights = jax.nn.softmax(jnp.stack(lses, axis=2), axis=2)
    return jnp.sum(weights[..., None] * outs, axis=2).reshape(bsz, seq, MIX_W).astype(q.dtype)


def cross_attention(h, mem_n, w_q, w_kv, w_o):
    bsz, seq, _ = h.shape
    q = (h @ w_q).reshape(bsz, seq, X_HEADS, X_HEAD_DIM)
    k, v = jnp.split(mem_n @ w_kv, 2, axis=-1)
    k = k.reshape(bsz, MEM_LEN, X_HEADS, X_HEAD_DIM)
    v = v.reshape(bsz, MEM_LEN, X_HEADS, X_HEAD_DIM)
    scores = jnp.einsum('bshd,bmhd->bhsm', q, k).astype(jnp.float32) * X_HEAD_DIM ** -0.5
    probs = jax.nn.softmax(scores, axis=-1).astype(v.dtype)
    o = jnp.einsum('bhsm,bmhd->bshd', probs, v).reshape(bsz, seq, D_MODEL)
    return o @ w_o


def squared_relu_mlp(h, w_up, w_down):
    return jnp.square(jax.nn.relu(h @ w_up)) @ w_down


def setup_inputs(seed: int = 0) -> dict:
    key = jax.random.key(seed)
    ks = jax.random.split(key, 32)
    f32 = jnp.float32

    def nrm(k, shape, scale):
        return jax.random.normal(k, shape, f32) * scale

    def gain(k, shape):
        return 1.0 + 0.01 * jax.random.normal(k, shape, f32)

    return {
        'x': nrm(ks[0], (BATCH, SEQ, D_MODEL), 1.0),
        'mem': nrm(ks[1], (BATCH, MEM_LEN, D_MODEL), 1.0),
        'g_mix': gain(ks[2], (DEPTH, D_MODEL)),
        'w_in': nrm(ks[3], (DEPTH, D_MODEL, D_IN), D_MODEL ** -0.5),
        's5_a_re': -0.5 + nrm(ks[4], (DEPTH, S5_GROUPS, S5_STATE), 0.01),
        's5_a_im': math.pi * jnp.arange(S5_STATE, dtype=f32)[None, None, :] + nrm(ks[5], (DEPTH, S5_GROUPS, S5_STATE), 0.01),
        's5_log_step': jax.random.uniform(ks[6], (DEPTH, S5_GROUPS), f32, math.log(1e-3), math.log(1e-1)),
        's5_b_re': nrm(ks[7], (DEPTH, S5_GROUPS, S5_STATE, S5_GROUP_CH), (2 * S5_GROUP_CH) ** -0.5),
        's5_b_im': nrm(ks[8], (DEPTH, S5_GROUPS, S5_STATE, S5_GROUP_CH), (2 * S5_GROUP_CH) ** -0.5),
        's5_c_re': nrm(ks[9], (DEPTH, S5_GROUPS, S5_GROUP_CH, S5_STATE), S5_STATE ** -0.25),
        's5_c_im': nrm(ks[10], (DEPTH, S5_GROUPS, S5_GROUP_CH, S5_STATE), S5_STATE ** -0.25),
        's5_d': nrm(ks[11], (DEPTH, MIX_W), 1.0),
        'w_glu': nrm(ks[12], (DEPTH, MIX_W, MIX_W), MIX_W ** -0.5),
        'b_glu': nrm(ks[13], (DEPTH, MIX_W), 0.01),
        'w_gla_gate': nrm(ks[14], (DEPTH, GLA_GATE_RANK, GLA_HEADS * GLA_DK), GLA_GATE_RANK ** -0.5),
        'b_gla_gate': nrm(ks[15], (DEPTH, GLA_HEADS * GLA_DK), 0.01),
        'g_gla_out': gain(ks[16], (DEPTH, GLA_DV)),
        'w_branch': nrm(ks[17], (DEPTH, N_BRANCH, MIX_W, D_MODEL), MIX_W ** -0.5),
        'w_out': nrm(ks[18], (DEPTH, D_MODEL, D_MODEL), D_MODEL ** -0.5),
        'g_mem': gain(ks[19], (D_MODEL,)),
        'g_cross': gain(ks[20], (DEPTH, D_MODEL)),
        'w_xq': nrm(ks[21], (DEPTH, D_MODEL, D_MODEL), D_MODEL ** -0.5),
        'w_xkv': nrm(ks[22], (DEPTH, D_MODEL, 2 * D_MODEL), D_MODEL ** -0.5),
        'w_xo': nrm(ks[23], (DEPTH, D_MODEL, D_MODEL), D_MODEL ** -0.5),
        'g_mlp': gain(ks[24], (DEPTH, D_MODEL)),
        'w_up': nrm(ks[25], (DEPTH, D_MODEL, D_FF), D_MODEL ** -0.5),
        'w_down': nrm(ks[26], (DEPTH, D_FF, D_MODEL), D_FF ** -0.5),
        'g_final': gain(ks[27], (D_MODEL,)),
    }


def reference(x, mem, g_mix, w_in, s5_a_re, s5_a_im, s5_log_step, s5_b_re, s5_b_im, s5_c_re, s5_c_im, s5_d,
              w_glu, b_glu, w_gla_gate, b_gla_gate, g_gla_out, w_branch, w_out, g_mem, g_cross, w_xq, w_xkv,
              w_xo, g_mlp, w_up, w_down, g_final):
    bsz, seq, _ = x.shape
    mem_n = rms_norm(mem, g_mem)
    split_points = [int(p) for p in np.cumsum(PROJ_WIDTHS)[:-1]]
    for l in range(DEPTH):
        h = rms_norm(x, g_mix[l])
        (u_s5, q_gla, k_gla, v_gla, lr_gla, r_gla, q_att, k_att, v_att, gate_logits) = jnp.split(h @ w_in[l], split_points, axis=-1)
        y_a = s5_branch(u_s5, s5_a_re[l], s5_a_im[l], s5_log_step[l], s5_b_re[l], s5_b_im[l], s5_c_re[l], s5_c_im[l],
                        s5_d[l], w_glu[l], b_glu[l])
        y_b = gla_branch(q_gla, k_gla, v_gla, lr_gla, r_gla, w_gla_gate[l], b_gla_gate[l], g_gla_out[l])
        y_c = dilated_branch(q_att, k_att, v_att)
        branches = jnp.stack([y_a, y_b, y_c], axis=2)
        gates = jax.nn.sigmoid(gate_logits.reshape(bsz, seq, N_BRANCH, D_MODEL))
        merged = jnp.sum(gates * jnp.einsum('bsnc,ncd->bsnd', branches, w_branch[l]), axis=2)
        x = x + merged @ w_out[l]
        x = x + cross_attention(rms_norm(x, g_cross[l]), mem_n, w_xq[l], w_xkv[l], w_xo[l])
        x = x + squared_relu_mlp(rms_norm(x, g_mlp[l]), w_up[l], w_down[l])
    return rms_norm(x, g_final)
```

```python
import math
import contextlib
import numpy as np
import concourse.bass as bass
import concourse.mybir as mybir
from concourse.bass_utils import run_bass_kernel_spmd

F32 = mybir.dt.float32
BF16 = mybir.dt.bfloat16
I32 = mybir.dt.int32
AF = mybir.ActivationFunctionType
ALU = mybir.AluOpType

SAME_ENGINE_SYNC = True
NOSYNC_ENGINES = ("pe",)

S = 4096
D = 1024
NT = S // 128
NCH = S // 512
KD = D // 128
DEPTH = 2
MEM = 256
D_IN = 9744
OFF_U, OFF_QG, OFF_KG, OFF_VG, OFF_LR, OFF_RG, OFF_QA, OFF_KA, OFF_VA, OFF_GT = (
    0, 512, 768, 1024, 1536, 1552, 2064, 3600, 5136, 6672)
TWO_PI = 2.0 * math.pi
SIN_SCALE = 6.2831
S5E = 'dve,pool,pool,dve'.split(',')


class Buf:
    __slots__ = ("name", "writers", "readers", "war")

    def __init__(self, name):
        self.name = name
        self.writers = []
        self.readers = []
        self.war = []


class Op:
    __slots__ = ("eng", "fn", "deps", "ref", "val", "semkey", "is_dma")

    def __init__(self, eng, fn, is_dma=False, semkey=None):
        self.eng = eng
        self.fn = fn
        self.deps = []
        self.ref = False
        self.val = None
        self.semkey = semkey
        self.is_dma = is_dma


class Tile:
    def __init__(self, t, b):
        self.t = t
        self.b = b

    def __getitem__(self, k):
        return self.t[k]


class Sched:
    ENG = ("pe", "dve", "act", "pool", "sp")

    def __init__(self, nc):
        self.nc = nc
        self.ops = {e: [] for e in self.ENG}
        self.all_ops = []
        self.stack = contextlib.ExitStack()
        self._n = 0
        self.pending = {e: [] for e in self.ENG}
        self.last = {e: None for e in self.ENG}
        self.open_dmas = []
        self.last_dma = {}

    def sb(self, shape, dtype, name=None, stack=None):
        self._n += 1
        nm = f"{name or 'sb'}_{self._n}"
        t = (stack or self.stack).enter_context(self.nc.sbuf_tensor(nm, list(shape), dtype))
        return Tile(t, Buf(nm))

    def ps(self, shape, dtype, name=None, stack=None):
        self._n += 1
        nm = f"{name or 'ps'}_{self._n}"
        t = (stack or self.stack).enter_context(self.nc.psum_tensor(nm, list(shape), dtype))
        return Tile(t, Buf(nm))

    def buf(self, name=None):
        self._n += 1
        return Buf(f"{name or 'b'}_{self._n}")

    def op(self, eng, fn, reads=(), writes=(), dma_key=None, partial=False):
        is_dma = dma_key is not None
        o = Op(eng, fn, is_dma=is_dma, semkey=dma_key if is_dma else eng)
        reads = [r.b if isinstance(r, Tile) else r for r in reads]
        writes = [w.b if isinstance(w, Tile) else w for w in writes]
        deps = list(self.pending[eng])
        self.pending[eng] = []
        if is_dma:
            prev = self.last_dma.get(dma_key)
            if prev is not None:
                deps.append(prev)
        for b in reads:
            deps.extend(b.writers)
        for b in writes:
            if partial and not b.readers:
                deps.extend(b.war)
            else:
                deps.extend(b.writers)
                deps.extend(b.readers)
        seen = set()
        for d in deps:
            if d is None or id(d) in seen or d is o:
                continue
            seen.add(id(d))
            if (not d.is_dma) and (not is_dma) and d.eng == eng:
                if eng in NOSYNC_ENGINES or not SAME_ENGINE_SYNC:
                    continue
            o.deps.append(d)
            d.ref = True
        for b in writes:
            if partial and not b.readers:
                b.writers = b.writers + [o]
            else:
                b.war = list(b.readers) if partial else []
                b.writers = [o]
                b.readers = []
        for b in reads:
            if b not in writes:
                b.readers.append(o)
        self.ops[eng].append(o)
        self.all_ops.append(o)
        if is_dma:
            self.open_dmas.append(o)
            self.last_dma[dma_key] = o
        else:
            self.last[eng] = o
        return o

    def dma(self, out, in_, reads=(), writes=(), key="dma", eng="sp", partial=False, **kw):
        return self.op(eng, lambda e: e.dma_start(out=out, in_=in_, **kw), reads, writes, dma_key=key, partial=partial)

    def wload(self, out, in_, writes, partial=True):
        self._wl = getattr(self, "_wl", 0) + 1
        return self.dma(out, in_, writes=writes, key=f"wl{self._wl % 12}", eng="pool", partial=partial)

    def barrier(self):
        deps = [self.last[e] for e in self.ENG if self.last[e] is not None] + list(self.open_dmas)
        for e in self.ENG:
            self.pending[e] = list(deps) + self.pending[e]
        self.open_dmas = []

    def emit(self, final_wait_ops=()):
        nc = self.nc
        counters = {}
        fset = set(id(o) for o in final_wait_ops)
        for o in self.all_ops:
            if id(o) in fset or o.is_dma:
                o.ref = True
        for o in self.all_ops:
            if o.ref:
                step = 16 if o.is_dma else 1
                counters[o.semkey] = counters.get(o.semkey, 0) + step
                o.val = counters[o.semkey]
        self.counters = counters
        sems = {}
        for k in counters:
            sems[k] = self.stack.enter_context(nc.semaphore(f"s_{k}"))
        block = self.stack.enter_context(nc.Block())
        engmap = {"pe": block.tensor, "dve": block.vector, "act": block.scalar,
                  "pool": block.gpsimd, "sp": block.sync}
        stats = {"waits": 0}
        for ename in self.ENG:
            ops = self.ops[ename]
            if not ops and not (ename == "sp" and final_wait_ops):
                continue

            def body(e, ops=ops, ename=ename):
                seen = {}
                for o in ops:
                    need = {}
                    for d in o.deps:
                        if seen.get(d.semkey, 0) >= d.val:
                            continue
                        if need.get(d.semkey, 0) < d.val:
                            need[d.semkey] = d.val
                    for k, v in need.items():
                        e.wait_ge(sems[k], v)
                        seen[k] = v
                        stats["waits"] += 1
                    ins = o.fn(e)
                    if o.ref:
                        ins.then_inc(sems[o.semkey], 16 if o.is_dma else 1)
                if ename == "sp":
                    for o in final_wait_ops:
                        if seen.get(o.semkey, 0) < o.val:
                            e.wait_ge(sems[o.semkey], o.val)
                            seen[o.semkey] = o.val

            engmap[ename](body)
        self.nwaits = stats["waits"]

    def close(self):
        self.stack.close()


class K:
    def __init__(self, nc):
        self.nc = nc
        self.s = Sched(nc)
        self.rr = 0
        self.dbg_out = None

    def tt(self, eng, out, in0, in1, op, reads, writes):
        return self.s.op(eng, lambda e: e.tensor_tensor(out=out, in0=in0, in1=in1, op=op), reads, writes)

    def ts(self, eng, out, in0, s1, s2, op0, op1, reads, writes):
        if s2 is None:
            return self.s.op(eng, lambda e: e.tensor_scalar(out=out, in0=in0, scalar1=s1, scalar2=None, op0=op0), reads, writes)
        return self.s.op(eng, lambda e: e.tensor_scalar(out=out, in0=in0, scalar1=s1, scalar2=s2, op0=op0, op1=op1), reads, writes)

    def stt(self, eng, out, in0, scalar, in1, op0, op1, reads, writes):
        return self.s.op(eng, lambda e: e.scalar_tensor_tensor(out=out, in0=in0, scalar=scalar, in1=in1, op0=op0, op1=op1), reads, writes)

    def act(self, out, in_, func, reads, writes, bias=None, scale=None, accum_out=None):
        kw = {}
        if bias is not None:
            kw["bias"] = bias
        if scale is not None:
            kw["scale"] = scale
        if accum_out is not None:
            kw["accum_out"] = accum_out
        return self.s.op("act", lambda e: e.activation(out=out, in_=in_, func=func, **kw), reads, writes)

    def copy(self, eng, out, in_, reads, writes):
        if eng == "act":
            return self.s.op("act", lambda e: e.activation(out=out, in_=in_, func=AF.Copy), reads, writes)
        return self.s.op(eng, lambda e: e.tensor_copy(out=out, in_=in_), reads, writes)

    def mm(self, out, lhsT, rhs, start, stop, reads, writes):
        return self.s.op("pe", lambda e: e.matmul(out, lhsT=lhsT, rhs=rhs, start=start, stop=stop), reads, writes)

    def tr(self, out, in_, ident, reads, writes):
        return self.s.op("pe", lambda e: e.transpose(out, in_, ident), reads, writes)

    def memset(self, eng, ap, val, writes):
        return self.s.op(eng, lambda e: e.memset(ap, val), [], writes)

    def alt(self):
        self.rr ^= 1
        return "dve" if self.rr else "act"


def make_consts(k):
    s = k.s
    c = {}
    c["ident"] = s.sb([128, 128], BF16, "ident")
    c["identf"] = s.sb([128, 128], F32, "identf")
    c["swapf"] = s.sb([128, 128], F32, "swapf")
    c["iota"] = s.sb([128, 512], F32, "iota")
    c["ones"] = s.sb([128, 128], BF16, "ones")
    c["sgn"] = s.sb([128, 1], F32, "sgn")
    tst = contextlib.ExitStack()
    tmpf = s.sb([128, 128], F32, "tmpf", tst)
    iotai = s.sb([128, 512], I32, "iotai", tst)
    k.memset("pool", c["ident"][:], 1.0, [c["ident"]])
    s.op("pool", lambda e: e.affine_select(out=c["ident"][:], in_=c["ident"][:], pattern=[[-1, 128]],
                                           compare_op=ALU.is_equal, fill=0.0, base=0, channel_multiplier=1),
         [c["ident"]], [c["ident"]])
    k.copy("pool", c["identf"][:], c["ident"][:], [c["ident"]], [c["identf"]])
    k.memset("pool", c["swapf"][:], 1.0, [c["swapf"]])
    k.memset("pool", tmpf[:], 1.0, [tmpf])
    s.op("pool", lambda e: e.affine_select(out=c["swapf"][:], in_=c["swapf"][:], pattern=[[1, 128]],
                                           compare_op=ALU.is_equal, fill=0.0, base=-64, channel_multiplier=-1),
         [c["swapf"]], [c["swapf"]])
    s.op("pool", lambda e: e.affine_select(out=tmpf[:], in_=tmpf[:], pattern=[[1, 128]],
                                           compare_op=ALU.is_equal, fill=0.0, base=64, channel_multiplier=-1),
         [tmpf], [tmpf])
    k.tt("pool", c["swapf"][:], c["swapf"][:], tmpf[:], ALU.add, [c["swapf"], tmpf], [c["swapf"]])
    s.op("pool", lambda e: e.iota(iotai[:], pattern=[[1, 512]], base=0, channel_multiplier=0), [], [iotai])
    k.copy("pool", c["iota"][:], iotai[:], [iotai], [c["iota"]])
    k.memset("pool", c["ones"][:], 1.0, [c["ones"]])
    k.memset("pool", c["sgn"][:], 1.0, [c["sgn"]])
    s.op("pool", lambda e: e.affine_select(out=c["sgn"][:], in_=c["sgn"][:], pattern=[[0, 1]],
                                           compare_op=ALU.is_ge, fill=-1.0, base=63, channel_multiplier=-1),
         [c["sgn"]], [c["sgn"]])
    s.barrier()
    tst.close()
    return c


class WStream:
    def __init__(self, k, nk, ncols, st, nbuf=2, cast_eng="pool"):
        self.k = k
        self.nbuf = nbuf
        self.wb = [k.s.sb([128, nk, ncols], BF16, "wbf", st) for _ in range(nbuf)]
        self.i = 0

    def load(self, w_ap):
        i = self.i % self.nbuf
        self.i += 1
        n = w_ap.shape[1]
        self.k.s.wload(self.wb[i][:, :, 0:n], w_ap.rearrange("(k p) c -> p k c", p=128), [self.wb[i]], partial=False)
        return self.wb[i]


def linear_fm(k, ws, w_dram, col0, ncol, xT, nk, psums, evac, slab=256, ntok=S):
    slabs = [(c0, min(slab, ncol - c0)) for c0 in range(0, ncol, slab)]
    nxt = ws.load(w_dram[:, col0 + slabs[0][0]: col0 + slabs[0][0] + slabs[0][1]])
    pi = 0
    for si, (c0, n) in enumerate(slabs):
        cur = nxt
        if si + 1 < len(slabs):
            c1, n1 = slabs[si + 1]
            nxt = ws.load(w_dram[:, col0 + c1: col0 + c1 + n1])
        for mo in range(0, n, 128):
            mw = min(128, n - mo)
            m = (c0 + mo) // 128
            for tc in range(ntok // 512):
                ps = psums[pi % len(psums)]
                pi += 1
                for kk in range(nk):
                    k.mm(ps[0:mw, :], cur[:, kk, mo:mo + mw], xT[:, kk, tc * 512:(tc + 1) * 512],
                         kk == 0, kk == nk - 1, [cur, xT], [ps])
                evac(m, tc, ps, mw)


def phase_norm(k, c, x_dram, gcol, hT_dram, ntiles=NT):
    s = k.s
    with contextlib.ExitStack() as st:
        xb = [s.sb([128, D], F32, "xb", st) for _ in range(3)]
        junk = s.sb([128, D], BF16, "junk", st)
        xs = [s.sb([128, D], BF16, "xs", st) for _ in range(3)]
        ss = s.sb([128, ntiles], F32, "ss", st)
        rs = s.sb([128, ntiles], F32, "rs", st)
        ho = [s.sb([128, KD, 512], BF16, "ho", st) for _ in range(2)]
        pst = [s.ps([128, KD, 128], BF16, "pst", st) for _ in range(2)]
        ssb = [s.buf("ssb") for _ in range(ntiles)]
        rsb = [s.buf("rsb") for _ in range(ntiles)]
        k.memset("pool", ss[:], 0.0, ssb)
        def st1a(t):
            x_t = xb[t % 3]
            s.dma(x_t[:], x_dram[t * 128:(t + 1) * 128, :], writes=[x_t], key=f"xb{t % 3}")
            k.act(junk[:], x_t[:], AF.Square, [x_t], [junk, ssb[t]], accum_out=ss[:, t:t + 1])
            k.act(rs[:, t:t + 1], ss[:, t:t + 1], AF.Sqrt, [ssb[t]], [rsb[t]], bias=1e-6, scale=1.0 / D)
            s.op("dve", lambda e, t=t: e.reciprocal(out=rs[:, t:t + 1], in_=rs[:, t:t + 1]), [rsb[t]], [rsb[t]])
            xs_t = xs[t % 3]
            k.ts("dve", xs_t[:], x_t[:], rs[:, t:t + 1], None, ALU.mult, None, [x_t, rsb[t]], [xs_t])

        def st1b(t):
            xs_t = xs[t % 3]
            p = pst[t % 2]
            for kk in range(KD):
                k.tr(p[:, kk, :], xs_t[:, kk * 128:(kk + 1) * 128], c["ident"][:], [xs_t, c["ident"]], [p])

        def st2(t):
            p = pst[t % 2]
            h = ho[(t // 4) % 2]
            k.tt("dve", h[:, :, (t % 4) * 128:(t % 4 + 1) * 128], p[:],
                 gcol.unsqueeze(2).to_broadcast([128, KD, 128]), ALU.mult, [p], [h])
            if t % 4 == 3:
                tc = t // 4
                s.dma(hT_dram[:, :, tc * 512:(tc + 1) * 512], h[:], reads=[h], key=f"ho{tc % 2}", eng="act")
        st1a(0)
        if ntiles > 1:
            st1a(1)
        st1b(0)
        for t in range(ntiles):
            if t + 2 < ntiles:
                st1a(t + 2)
            if t + 1 < ntiles:
                st1b(t + 1)
            st2(t)
        s.barrier()


def frac_sincos(k, st, t_turns, n, reads, name="sc"):
    s = k.s
    ti = s.sb([128, n], I32, name + "_i", st)
    f = s.sb([128, n], F32, name + "_f", st)
    sn = s.sb([128, n], F32, name + "_s", st)
    cs = s.sb([128, n], F32, name + "_c", st)
    k.copy("dve", ti[:], t_turns[:], reads, [ti])
    k.tt("dve", f[:], t_turns[:], ti[:], ALU.subtract, reads + [ti], [f])
    k.act(sn[:], f[:], AF.Sin, [f], [sn], scale=SIN_SCALE)
    k.ts("dve", f[:], t_turns[:], 0.25, None, ALU.add, None, reads, [f])
    k.copy("dve", ti[:], f[:], [f], [ti])
    k.tt("dve", f[:], f[:], ti[:], ALU.subtract, [f, ti], [f])
    k.act(cs[:], f[:], AF.Sin, [f], [cs], scale=SIN_SCALE)
    return sn, cs


def phase_s5(k, c, hT_dram, w_in_l, p5, w_glu_l, bglu_col, yaT_dram):
    s = k.s
    Stager._id = 0
    WStream._id = 0
    with contextlib.ExitStack() as st0:
        Bw = s.sb([128, 16, 2, 128], BF16, "Bw", st0)
        Cp1 = s.sb([128, 32, 64], BF16, "Cp1", st0)
        Cp2 = s.sb([128, 32, 64], BF16, "Cp2", st0)
        Dmp = s.sb([128, 16, 64], BF16, "Dmp", st0)
        rcol = s.sb([128, 32], F32, "rcol", st0)
        thcol = s.sb([128, 32], F32, "thcol", st0)
        Rg = s.sb([128, 32, 128], F32, "Rg", st0)
        yT = s.sb([128, 4, S], BF16, "yT", st0)
        with contextlib.ExitStack() as st:
            n = 1024

            def ld(name, ap, shape):
                t = s.sb(shape, F32, name, st)
                s.dma(t[:], ap, writes=[t], key="s5" + name)
                return t
            bre = ld("bre", p5["bre"].rearrange("p a b -> p (a b)"), [128, n])
            bim = ld("bim", p5["bim"].rearrange("p a b -> p (a b)"), [128, n])
            ar = ld("ar", p5["ar"].rearrange("p a b -> p (a b)"), [128, n])
            ai = ld("ai", p5["ai"].rearrange("p a b -> p (a b)"), [128, n])
            ls = ld("ls", p5["ls"].rearrange("p a b -> p (a b)"), [128, n])
            stt = ld("stt", p5["st"], [128, 3, 32])
            c1 = ld("c1", p5["c1"], [128, 32, 16])
            c2 = ld("c2", p5["c2"], [128, 32, 16])
            dm = ld("dm", p5["dm"], [128, 16, 64])
            t1 = s.sb([128, n], F32, "t1", st)
            t2 = s.sb([128, n], F32, "t2", st)
            t3 = s.sb([128, n], F32, "t3", st)
            t4 = s.sb([128, n], F32, "t4", st)
            tq = s.sb([128, n], F32, "tq", st)
            k.act(t1[:], ls[:], AF.Exp, [ls], [t1])
            k.tt("dve", tq[:], ai[:], t1[:], ALU.mult, [ai, t1], [tq])
            k.ts("dve", tq[:], tq[:], 1.0 / TWO_PI, None, ALU.mult, None, [tq], [tq])
            k.tt("dve", t1[:], ar[:], t1[:], ALU.mult, [ar, t1], [t1])
            k.act(t1[:], t1[:], AF.Exp, [t1], [t1])
            sn, cs = frac_sincos(k, st, tq, n, [tq], "sc1")
            lbi = t2
            nr = t3
            k.tt("dve", lbi[:], t1[:], sn[:], ALU.mult, [t1, sn], [lbi])
            k.tt("dve", nr[:], t1[:], cs[:], ALU.mult, [t1, cs], [nr])
            k.ts("dve", nr[:], nr[:], -1.0, None, ALU.add, None, [nr], [nr])
            k.tt("dve", t1[:], ar[:], ar[:], ALU.mult, [ar], [t1])
            k.tt("dve", t4[:], ai[:], ai[:], ALU.mult, [ai], [t4])
            k.tt("dve", t1[:], t1[:], t4[:], ALU.add, [t1, t4], [t1])
            s.op("dve", lambda e: e.reciprocal(out=t1[:], in_=t1[:]), [t1], [t1])
            cr = sn
            ci = cs
            k.tt("dve", cr[:], nr[:], ar[:], ALU.mult, [nr, ar], [cr])
            k.tt("dve", t4[:], lbi[:], ai[:], ALU.mult, [lbi, ai], [t4])
            k.tt("dve", cr[:], cr[:], t4[:], ALU.add, [cr, t4], [cr])
            k.tt("dve", cr[:], cr[:], t1[:], ALU.mult, [cr, t1], [cr])
            k.tt("dve", ci[:], lbi[:], ar[:], ALU.mult, [lbi, ar], [ci])
            k.tt("dve", t4[:], nr[:], ai[:], ALU.mult, [nr, ai], [t4])
            k.tt("dve", ci[:], ci[:], t4[:], ALU.subtract, [ci, t4], [ci])
            k.tt("dve", ci[:], ci[:], t1[:], ALU.mult, [ci, t1], [ci])
            k.tt("dve", t2[:], cr[:], bre[:], ALU.mult, [cr, bre], [t2])
            k.tt("dve", t4[:], ci[:], bim[:], ALU.mult, [ci, bim], [t4])
            k.tt("dve", t2[:], t2[:], t4[:], ALU.subtract, [t2, t4], [t2])
            k.tt("dve", t3[:], cr[:], bim[:], ALU.mult, [cr, bim], [t3])
            k.tt("dve", t4[:], ci[:], bre[:], ALU.mult, [ci, bre], [t4])
            k.tt("dve", t3[:], t3[:], t4[:], ALU.add, [t3, t4], [t3])
            bbr3 = t2[:].rearrange("p (a b) -> p a b", b=64)
            bbi3 = t3[:].rearrange("p (a b) -> p a b", b=64)
            k.copy("dve", Bw[:, :, 0, 0:64], bbr3, [t2], [Bw])
            k.copy("dve", Bw[:, :, 0, 64:128], bbi3, [t3], [Bw])
            k.copy("dve", Bw[:, :, 1, 0:64], bbi3, [t3], [Bw])
            k.ts("dve", Bw[:, :, 1, 64:128], bbr3, -1.0, None, ALU.mult, None, [t2], [Bw])
            k.memset("pool", Cp1[:], 0.0, [Cp1])
            k.memset("pool", Cp2[:], 0.0, [Cp2])
            cp1v = Cp1[:].rearrange("p (a q e) w -> p a q e w", a=4, q=4, e=2)
            cp2v = Cp2[:].rearrange("p (a q e) w -> p a q e w", a=4, q=4, e=2)
            c1v = c1[:].rearrange("p (a q e) w -> p a q e w", a=4, q=4, e=2)
            c2v = c2[:].rearrange("p (a q e) w -> p a q e w", a=4, q=4, e=2)
            for e_ in range(2):
                for (q0, q1, off) in ((0, 3, 16 * e_), (3, 4, 32 + 16 * e_)):
                    k.ts("dve", cp1v[:, :, q0:q1, e_, off:off + 16], c1v[:, :, q0:q1, e_, :], c["sgn"][:, 0:1], None,
                         ALU.mult, None, [c1, c["sgn"]], [Cp1])
                    k.ts("dve", cp2v[:, :, q0:q1, e_, off:off + 16], c2v[:, :, q0:q1, e_, :], -1.0, None,
                         ALU.mult, None, [c2], [Cp2])
            k.copy("dve", Dmp[:], dm[:], [dm], [Dmp])
            u1 = s.sb([128, 32], F32, "u1", st)
            u2 = s.sb([128, 32], F32, "u2", st)
            k.act(u1[:], stt[:, 2, :], AF.Exp, [stt], [u1])
            k.tt("dve", u2[:], stt[:, 0, :], u1[:], ALU.mult, [stt, u1], [u2])
            k.act(rcol[:], u2[:], AF.Exp, [u2], [rcol])
            k.tt("dve", thcol[:], stt[:, 1, :], u1[:], ALU.mult, [stt, u1], [thcol])
            k.ts("dve", thcol[:], thcol[:], 1.0 / TWO_PI, None, ALU.mult, None, [thcol], [thcol])
            k.ts("dve", u2[:], thcol[:], 512.0, None, ALU.mult, None, [thcol], [u2])
            s512, c512 = frac_sincos(k, st, u2, 32, [u2], "sc2")
            k.ts("dve", s512[:], s512[:], c["sgn"][:, 0:1], None, ALU.mult, None, [s512, c["sgn"]], [s512])
            for g in range(32):
                k.ts("dve", Rg[:, g, :], c["identf"][:], c512[:, g:g + 1], None, ALU.mult, None,
                     [c["identf"], c512], [Rg])
                k.stt("dve", Rg[:, g, :], c["swapf"][:], s512[:, g:g + 1], Rg[:, g, :], ALU.mult, ALU.add,
                      [c["swapf"], s512, Rg], [Rg])
            s.barrier()
        with contextlib.ExitStack() as st:
            uT = s.sb([128, 4, S], BF16, "uT", st)
            with contextlib.ExitStack() as st2:
                ws = WStream(k, KD, 256, st2)
                wu = [ws.load(w_in_l[:, OFF_U + i * 256: OFF_U + (i + 1) * 256]) for i in range(2)]
                hc = [s.sb([128, KD, 512], BF16, "hc", st2) for _ in range(2)]
                psu = [s.ps([128, 512], F32, "psu", st2) for _ in range(4)]
                pi = 0
                s.dma(hc[0][:], hT_dram[:, :, 0:512], writes=[hc[0]], key="hc0")
                for tc in range(NCH):
                    h = hc[tc % 2]
                    if tc + 1 < NCH:
                        s.dma(hc[(tc + 1) % 2][:], hT_dram[:, :, (tc + 1) * 512:(tc + 2) * 512], writes=[hc[(tc + 1) % 2]], key=f"hc{(tc + 1) % 2}")
                    for m in range(4):
                        ps = psu[pi % 4]
                        pi += 1
                        w = wu[m // 2]
                        mo = (m % 2) * 128
                        for kk in range(KD):
                            k.mm(ps[:], w[:, kk, mo:mo + 128], h[:, kk, :], kk == 0, kk == KD - 1, [w, h], [ps])
                        k.copy(k.alt(), uT[:, m, tc * 512:(tc + 1) * 512], ps[:], [ps], [uT])
                s.barrier()
            NSLOT = 2
            tabs = [[s.sb([128, 512], F32, "tab", st) for _ in range(2)] for _ in range(2 * NSLOT)]
            ttmp = s.sb([128, 512], F32, "ttmp", st)
            tti = s.sb([128, 512], I32, "tti", st)
            tf = s.sb([128, 512], F32, "tf", st)
            T1 = [[s.sb([128, 512], F32, "T1", st) for _ in range(2)] for _ in range(NSLOT)]
            T2 = [[s.sb([128, 512], F32, "T2", st) for _ in range(2)] for _ in range(NSLOT)]
            X = [[s.sb([128, 512], F32, "X", st) for _ in range(2)] for _ in range(NSLOT)]
            G = [[s.sb([128, 512], F32, "G", st) for _ in range(2)] for _ in range(NSLOT)]
            G1 = [[s.sb([128, 512], BF16, "G1", st) for _ in range(2)] for _ in range(NSLOT)]
            G2 = [[s.sb([128, 512], BF16, "G2", st) for _ in range(2)] for _ in range(NSLOT)]
            psP = [[s.ps([128, 512], F32, "psP", st) for _ in range(2)] for _ in range(NSLOT)]
            psY = [s.ps([128, 512], F32, "psY", st) for _ in range(2)]
            psH = [s.ps([128, 512], F32, "psH", st) for _ in range(NSLOT)]
            hcol = [[s.sb([128, 1], F32, "hcol", st) for _ in range(2)] for _ in range(NSLOT)]
            tb = {}

            def gen_tables(gp):
                for si in range(NSLOT):
                    g = gp * NSLOT + si
                    sn_t, cs_t = tabs[(gp % 2) * NSLOT + si]
                    k.ts("dve", ttmp[:], c["iota"][:], thcol[:, g:g + 1], None, ALU.mult, None,
                         [c["iota"], thcol], [ttmp])
                    k.copy("dve", tti[:], ttmp[:], [ttmp], [tti])
                    k.tt("dve", tf[:], ttmp[:], tti[:], ALU.subtract, [ttmp, tti], [tf])
                    k.act(sn_t[:], tf[:], AF.Sin, [tf], [sn_t], scale=SIN_SCALE)
                    k.ts("dve", tf[:], ttmp[:], 0.25, None, ALU.add, None, [ttmp], [tf])
                    k.copy("dve", tti[:], tf[:], [tf], [tti])
                    k.tt("dve", tf[:], tf[:], tti[:], ALU.subtract, [tf, tti], [tf])
                    k.act(cs_t[:], tf[:], AF.Sin, [tf], [cs_t], scale=SIN_SCALE)
                    tb[g] = (sn_t, cs_t)

            def info(gp):
                out = []
                for si in range(NSLOT):
                    g = gp * NSLOT + si
                    kc, hb, j = g // 8, (g % 8) // 4, g % 4
                    out.append((si, g, kc, kc * 4 + j, 64 * hb, tb[g][0], tb[g][1]))
                return out

            def stage_ab(gp, ci_):
                b = ci_ % 2
                tk = slice(ci_ * 512, (ci_ + 1) * 512)
                gi_ = info(gp)
                for (si, g, kc, idx, r0, sn_t, cs_t) in gi_:
                    p1, p2 = psP[si]
                    k.mm(p1[:], Bw[r0:r0 + 64, idx, 0, :], uT[r0:r0 + 64, kc, tk], True, True, [Bw, uT], [p1])
                    k.mm(p2[:], Bw[r0:r0 + 64, idx, 1, :], uT[r0:r0 + 64, kc, tk], True, True, [Bw, uT], [p2])
                for (si, g, kc, idx, r0, sn_t, cs_t) in gi_:
                    p1, p2 = psP[si]
                    k.tt("dve", T1[si][b][:], p1[:], cs_t[:], ALU.mult, [p1, cs_t], [T1[si][b]])
                    k.tt("dve", T2[si][b][:], p2[:], sn_t[:], ALU.mult, [p2, sn_t], [T2[si][b]])
                for (si, g, kc, idx, r0, sn_t, cs_t) in gi_:
                    k.tt(S5E[0], X[si][b][:], T1[si][b][:], T2[si][b][:], ALU.add, [T1[si][b], T2[si][b]], [X[si][b]])
                for (si, g, kc, idx, r0, sn_t, cs_t) in gi_:
                    if ci_ == 0:
                        init = 0.0
                        rd = []
                    else:
                        hc_ = hcol[si][(ci_ - 1) % 2]
                        init = hc_[:, 0:1]
                        rd = [hc_]
                    s.op("dve", lambda e, o=G[si][b], x=X[si][b], g=g, init=init: e.tensor_tensor_scan(
                        out=o[:], data0=rcol[:, g:g + 1].to_broadcast([128, 512]), data1=x[:],
                        initial=init, op0=ALU.mult, op1=ALU.add), [X[si][b], rcol] + rd, [G[si][b]])
                for (si, g, kc, idx, r0, sn_t, cs_t) in gi_:
                    if ci_ < NCH - 1:
                        k.mm(psH[si][:, 0:1], Rg[:, g, :], G[si][b][:, 511:512], True, True,
                             [Rg, G[si][b]], [psH[si]])
                        k.copy("act", hcol[si][ci_ % 2][:], psH[si][:, 0:1], [psH[si]], [hcol[si][ci_ % 2]])

            def stage_c(gp, ci_):
                b = ci_ % 2
                tk = slice(ci_ * 512, (ci_ + 1) * 512)
                gi_ = info(gp)
                for n_, (si, g, kc, idx, r0, sn_t, cs_t) in enumerate(gi_):
                    k.tt(S5E[1], G1[si][b][:], G[si][b][:], cs_t[:], ALU.mult, [G[si][b], cs_t], [G1[si][b]])
                    e2 = S5E[2] if n_ == 0 else S5E[3]
                    k.tt(e2, G2[si][b][:], G[si][b][:], sn_t[:], ALU.mult, [G[si][b], sn_t], [G2[si][b]])
                q_ = gp % 4
                pbase, W = ((0, 32), (32, 32), (64, 64), (64, 64))[q_]
                py = psY[ci_ % 2]
                o = py[pbase:pbase + W, :]
                for n_, (si, g, kc, idx, r0, sn_t, cs_t) in enumerate(gi_):
                    k.mm(o, Cp1[:, g, 0:W], G1[si][b][:], n_ == 0, False, [G1[si][b], Cp1], [py])
                    k.mm(o, Cp2[:, g, 0:W], G2[si][b][:], False, False, [G2[si][b], Cp2], [py])
                (si, g, kc, idx, r0, sn_t, cs_t) = gi_[0]
                k.mm(o, Dmp[r0:r0 + 64, gp, 0:W], uT[r0:r0 + 64, kc, tk], False, True, [uT, Dmp], [py])
                if q_ < 3:
                    k.copy("act", yT[pbase:pbase + W, kc, tk], o, [py], [yT])
                else:
                    k.tt("dve", yT[64:128, kc, tk], o, yT[64:128, kc, tk], ALU.add, [py, yT], [yT])

            NP = 32 // NSLOT
            gen_tables(0)
            for gp in range(NP):
                for i_ in range(NCH + 1):
                    if i_ < NCH:
                        stage_ab(gp, i_)
                    if i_ >= 1:
                        stage_c(gp, i_ - 1)
                    if i_ == 3 and gp + 1 < NP:
                        gen_tables(gp + 1)
            s.barrier()
        with contextlib.ExitStack() as st:
            gT = s.sb([128, 4, S], BF16, "gT", st)
            gbufs = [[s.buf("gTb") for _ in range(NCH)] for _ in range(4)]
            for tc in range(NCH):
                for m in range(4):
                    k.act(gT[:, m, tc * 512:(tc + 1) * 512], yT[:, m, tc * 512:(tc + 1) * 512], AF.Gelu_apprx_tanh,
                          [yT], [gbufs[m][tc]])
            ws = WStream(k, 4, 512, st)
            wg = ws.load(w_glu_l)
            psz = [s.ps([128, 512], F32, "psz", st) for _ in range(2)]
            sg = [s.sb([128, 512], BF16, "sg", st) for _ in range(2)]
            yo = [s.sb([128, 512], BF16, "yo", st) for _ in range(2)]
            i = 0
            for tc in range(NCH):
                tk = slice(tc * 512, (tc + 1) * 512)
                for m in range(4):
                    ps = psz[i % 2]
                    for kk in range(4):
                        k.mm(ps[:], wg[:, kk, m * 128:(m + 1) * 128], gT[:, kk, tk], kk == 0, kk == 3, [wg, gbufs[kk][tc]], [ps])
                    k.act(sg[i % 2][:], ps[:], AF.Sigmoid, [ps], [sg[i % 2]], bias=bglu_col[:, m:m + 1], scale=1.0)
                    k.tt("dve", yo[i % 2][:], gT[:, m, tk], sg[i % 2][:], ALU.mult, [gbufs[m][tc], sg[i % 2]], [yo[i % 2]])
                    s.dma(yaT_dram[:, m, tk], yo[i % 2][:], reads=[yo[i % 2]], key=f"yo{i % 2}", eng="act")
                    i += 1
            s.barrier()


class Stager:
    def __init__(self, k, nk, ncols, st, nbuf=2, cast_eng="pool"):
        self.k = k
        self.i = 0

    def load(self, w_ap, dst_tile, dst_ap, rows=128):
        self.k.s.wload(dst_ap, w_ap.rearrange("(k p) c -> p k c", p=rows), [dst_tile])


def phase_gla(k, c, hT_dram, w_in_l, w_gate_l, b_gate_l, gout_col, ybT_dram, stop_after=9, sub=99):
    s = k.s
    Stager._id = 0
    WStream._id = 0
    with contextlib.ExitStack() as st0:
        qT = s.sb([128, 2, S], BF16, "qT", st0)
        kT = s.sb([128, 2, S], BF16, "kT", st0)
        k_tm = s.sb([128, NT, 256], BF16, "k_tm", st0)
        v_tm = s.sb([128, NT, 512], BF16, "v_tm", st0)
        lrT = s.sb([16, S], BF16, "lrT", st0)
        o_n = s.sb([128, NT, 512], BF16, "o_n", st0)
        triU = s.sb([128, 128], BF16, "triU", st0)
        triS = s.sb([128, 128], BF16, "triS", st0)
        m01 = s.sb([128, 128], BF16, "m01", st0)
        k.memset("pool", triU[:], -1.0 / 16.0, [triU])
        s.op("pool", lambda e: e.affine_select(out=triU[:], in_=triU[:], pattern=[[1, 128]], compare_op=ALU.is_ge,
                                               fill=0.0, base=0, channel_multiplier=-1), [triU], [triU])
        k.memset("pool", triS[:], -1.0 / 16.0, [triS])
        s.op("pool", lambda e: e.affine_select(out=triS[:], in_=triS[:], pattern=[[-1, 128]], compare_op=ALU.is_ge,
                                               fill=0.0, base=-1, channel_multiplier=1), [triS], [triS])
        k.memset("pool", m01[:], 1.0, [m01])
        s.op("pool", lambda e: e.affine_select(out=m01[:], in_=m01[:], pattern=[[1, 128]], compare_op=ALU.is_ge,
                                               fill=0.0, base=0, channel_multiplier=-1), [m01], [m01])
        wgate = s.sb([16, 256], BF16, "wgate", st0)
        bgate = s.sb([1, 256], BF16, "bgate", st0)
        with contextlib.ExitStack() as st:
            sg = Stager(k, KD, 256, st)
            wq = s.sb([128, KD, 256], BF16, "wq", st)
            wk = s.sb([128, KD, 256], BF16, "wk", st)
            wv = s.sb([128, KD, 512], BF16, "wv", st)
            wl = s.sb([128, KD, 16], BF16, "wl", st)
            sg.load(w_in_l[:, OFF_QG:OFF_QG + 256], wq, wq[:])
            sg.load(w_in_l[:, OFF_KG:OFF_KG + 256], wk, wk[:])
            sg.load(w_in_l[:, OFF_VG:OFF_VG + 256], wv, wv[:, :, 0:256])
            sg.load(w_in_l[:, OFF_VG + 256:OFF_VG + 512], wv, wv[:, :, 256:512])
            sg.load(w_in_l[:, OFF_LR:OFF_LR + 16], wl, wl[:])
            sg.load(w_gate_l, wgate, wgate[:].unsqueeze(1), rows=16)
            sg.load(b_gate_l, bgate, bgate[:].unsqueeze(1), rows=1)
            hc = [s.sb([128, KD, 512], BF16, "hc", st) for _ in range(2)]
            psf = [s.ps([128, 512], F32, "psf", st) for _ in range(3)]
            pst = [s.ps([128, 512], F32, "pst", st) for _ in range(3)]
            fi = 0
            ti = 0
            s.dma(hc[0][:], hT_dram[:, :, 0:512], writes=[hc[0]], key="ghc0")
            for tc in range(NCH):
                h = hc[tc % 2]
                tk = slice(tc * 512, (tc + 1) * 512)
                if tc + 1 < NCH:
                    s.dma(hc[(tc + 1) % 2][:], hT_dram[:, :, (tc + 1) * 512:(tc + 2) * 512], writes=[hc[(tc + 1) % 2]], key=f"ghc{(tc + 1) % 2}")
                for (w, dstT) in ((wq, qT), (wk, kT)):
                    for m in range(2):
                        ps = psf[fi % 3]
                        fi += 1
                        for kk in range(KD):
                            k.mm(ps[:], w[:, kk, m * 128:(m + 1) * 128], h[:, kk, :], kk == 0, kk == KD - 1, [w, h], [ps])
                        k.copy(k.alt(), dstT[:, m, tk], ps[:], [ps], [dstT])
                ps = psf[fi % 3]
                fi += 1
                for kk in range(KD):
                    k.mm(ps[0:16, :], wl[:, kk, :], h[:, kk, :], kk == 0, kk == KD - 1, [wl, h], [ps])
                k.copy(k.alt(), lrT[:, tk], ps[0:16, :], [ps], [lrT])
                for t in range(4):
                    T = tc * 4 + t
                    ps = pst[ti % 3]
                    ti += 1
                    for kk in range(KD):
                        k.mm(ps[:, 0:256], h[:, kk, t * 128:(t + 1) * 128], wk[:, kk, :], kk == 0, kk == KD - 1, [wk, h], [ps])
                    k.copy(k.alt(), k_tm[:, T, :], ps[:, 0:256], [ps], [k_tm])
                    ps = pst[ti % 3]
                    ti += 1
                    for kk in range(KD):
                        k.mm(ps[:], h[:, kk, t * 128:(t + 1) * 128], wv[:, kk, :], kk == 0, kk == KD - 1, [wv, h], [ps])
                    k.copy(k.alt(), v_tm[:, T, :], ps[:], [ps], [v_tm])
            s.barrier()
        if stop_after < 2:
            return
        with contextlib.ExitStack() as st:
            Sst = s.sb([128, 2, 128], F32, "Sst", st)
            Sbf = [s.sb([128, 2, 128], BF16, "Sbf", st) for _ in range(2)]
            e1 = [s.sb([128, 256], F32, "e1", st) for _ in range(2)]
            L = [s.sb([128, 256], F32, "L", st) for _ in range(2)]
            Lh = [s.sb([128, 256], BF16, "Lh", st) for _ in range(2)]
            Ll = [s.sb([128, 256], BF16, "Ll", st) for _ in range(2)]
            eb = [s.sb([128, 2, 128], F32, "eb", st) for _ in range(2)]
            enb = [s.sb([128, 2, 128], F32, "enb", st) for _ in range(2)]
            erc = [s.sb([128, 256], F32, "erc", st) for _ in range(2)]
            qe0 = [s.sb([128, 2, 128], BF16, "qe0", st) for _ in range(2)]
            qe1 = [s.sb([128, 2, 128], BF16, "qe1", st) for _ in range(2)]
            c0col = s.sb([128, 1], F32, "c0col", st)
            c1col = s.sb([128, 1], F32, "c1col", st)
            k.ts("dve", c0col[:], c["sgn"][:], 0.0625, 0.0625, ALU.mult, ALU.add, [c["sgn"]], [c0col])
            k.ts("dve", c1col[:], c["sgn"][:], -0.0625, 0.0625, ALU.mult, ALU.add, [c["sgn"]], [c1col])
            ke = [s.sb([128, 2, 128], BF16, "ke", st) for _ in range(2)]
            kend0 = [s.sb([128, 2, 128], BF16, "kend0", st) for _ in range(2)]
            kend1 = [s.sb([128, 2, 128], BF16, "kend1", st) for _ in range(2)]
            for q_ in range(2):
                k.memset("pool", kend0[q_][:], 0.0, [kend0[q_]])
                k.memset("pool", kend1[q_][:], 0.0, [kend1[q_]])
            Am = [s.sb([128, 4, 128], BF16, "Am", st) for _ in range(2)]
            junk = s.sb([128, 128], BF16, "gjunk", st)
            ssq = [s.sb([128, 4], F32, "ssq", st) for _ in range(2)]
            psZ = s.ps([128, 512], F32, "psZ", st)
            psR = s.ps([128, 512], F32, "psR", st)
            psB = s.ps([128, 4, 128], F32, "psB", st)
            psA = [s.ps([128, 4, 128], F32, "psA", st) for _ in range(2)]
            psO = [s.ps([128, 4, 128], F32, "psO", st) for _ in range(2)]
            psS = s.ps([128, 4, 128], F32, "psS", st)
            zb = s.buf("zb")
            rcb = s.buf("rcb")
            k.memset("pool", Sst[:], 0.0, [Sst])
            def f1(n):
                b = n % 2
                tk = slice(n * 128, (n + 1) * 128)
                k.mm(psZ[:, 0:256], lrT[0:16, tk], wgate[:], True, False, [lrT, wgate], [zb])
                k.mm(psZ[:, 0:256], c["ones"][0:1, :], bgate[:], False, True, [c["ones"], bgate], [zb])
                k.act(e1[b][:], psZ[:, 0:256], AF.Exp, [zb], [e1[b]], scale=-1.0)
                k.act(L[b][:], e1[b][:], AF.Ln, [e1[b]], [L[b]], bias=1.0)
                k.copy("dve", Lh[b][:], L[b][:], [L[b]], [Lh[b]])
                k.tt("dve", Ll[b][:], L[b][:], Lh[b][:], ALU.subtract, [L[b], Lh[b]], [Ll[b]])

            def f2(n):
                b = n % 2
                tk = slice(n * 128, (n + 1) * 128)
                for hp in range(2):
                    k.mm(psB[:, hp, :], Lh[b][:, hp * 128:(hp + 1) * 128], triU[:], True, False, [Lh[b], triU], [psB])
                    k.mm(psB[:, hp, :], Ll[b][:, hp * 128:(hp + 1) * 128], triU[:], False, True, [Ll[b], triU], [psB])
                k.mm(psR[:, 0:256], triS[:], Lh[b][:], True, False, [triS, Lh[b]], [rcb])
                k.mm(psR[:, 0:256], triS[:], Ll[b][:], False, True, [triS, Ll[b]], [rcb])
                k.act(eb[b][:], psB[:, 0:2, :], AF.Exp, [psB], [eb[b]])
                k.act(enb[b][:], psB[:, 0:2, :], AF.Exp, [psB], [enb[b]], scale=-1.0)
                k.act(erc[b][:], psR[:, 0:256], AF.Exp, [rcb], [erc[b]])
                k.stt("dve", qe0[b][:], qT[:, :, tk], c0col[:, 0:1], eb[b][:], ALU.mult, ALU.mult, [qT, eb[b], c0col], [qe0[b]])
                k.stt("dve", qe1[b][:], qT[:, :, tk], c1col[:, 0:1], eb[b][:], ALU.mult, ALU.mult, [qT, eb[b], c1col], [qe1[b]])
                k.tt("dve", ke[b][:], kT[:, :, tk], enb[b][:], ALU.mult, [kT, enb[b]], [ke[b]])
                kv = k_tm[:, n, :].rearrange("p (a b d) -> p a b d", a=2, b=2)
                ev = erc[b][:].rearrange("p (a b d) -> p a b d", a=2, b=2)
                k.tt("dve", kend0[b][:, :, 0:64], kv[:, :, 0, :], ev[:, :, 0, :], ALU.mult, [k_tm, erc[b]], [kend0[b]])
                k.tt("dve", kend1[b][:, :, 64:128], kv[:, :, 1, :], ev[:, :, 1, :], ALU.mult, [k_tm, erc[b]], [kend1[b]])

            def f3(n):
                b = n % 2
                qes = (qe0[b], qe1[b])
                pa = psA[b]
                for h in range(4):
                    hp = h // 2
                    k.mm(pa[:, h, :], ke[b][:, hp, :], qes[h % 2][:, hp, :], True, True, [ke[b], qes[h % 2]], [pa])
                k.tt("dve", Am[b][:], pa[:], m01[:].unsqueeze(1).to_broadcast([128, 4, 128]), ALU.mult, [pa, m01], [Am[b]])

            def b1(n):
                b = n % 2
                qes = (qe0[b], qe1[b])
                po = psO[b]
                sprev = Sbf[(n - 1) % 2]
                for h in range(4):
                    hp = h // 2
                    k.mm(po[:, h, :], Am[b][:, h, :], v_tm[:, n, h * 128:(h + 1) * 128], True, n == 0, [Am[b], v_tm], [po])
                    if n > 0:
                        k.mm(po[:, h, :], qes[h % 2][:, hp, :], sprev[:, hp, :], False, True,
                             [qes[h % 2], sprev], [po])

            def b2(n):
                b = n % 2
                if n < NT - 1:
                    for hp in range(2):
                        k.mm(psS[:, hp, :], kend0[b][:, hp, :], v_tm[:, n, (2 * hp) * 128:(2 * hp + 1) * 128],
                             True, False, [kend0[b], v_tm], [psS])
                        k.mm(psS[:, hp, :], kend1[b][:, hp, :], v_tm[:, n, (2 * hp + 1) * 128:(2 * hp + 2) * 128],
                             False, True, [kend1[b], v_tm], [psS])
                    for hp in range(2):
                        k.stt("dve", Sst[:, hp, :], Sst[:, hp, :], eb[b][:, hp, 127:128], psS[:, hp, :], ALU.mult, ALU.add,
                              [Sst, eb[b], psS], [Sst])
                    k.copy("pool", Sbf[n % 2][:], Sst[:], [Sst], [Sbf[n % 2]])

            def b3(n):
                b = n % 2
                po = psO[b]
                k.memset("pool", ssq[b][:], 0.0, [ssq[b]])
                for h in range(4):
                    k.act(junk[:], po[:, h, :], AF.Square, [po], [junk, ssq[b]], accum_out=ssq[b][:, h:h + 1])
                k.act(ssq[b][:], ssq[b][:], AF.Ln, [ssq[b]], [ssq[b]], bias=1e-6, scale=1.0 / 128.0)
                k.act(ssq[b][:], ssq[b][:], AF.Exp, [ssq[b]], [ssq[b]], scale=-0.5)
                k.tt("dve", o_n[:, n, :].rearrange("p (h e) -> p h e", e=128), po[:],
                     ssq[b][:].unsqueeze(2).to_broadcast([128, 4, 128]), ALU.mult, [po, ssq[b]], [o_n])

            f1(0)
            f2(0)
            f3(0)
            for n in range(NT):
                nx = n + 1 < NT
                if nx:
                    f1(n + 1)
                b1(n)
                if nx:
                    f2(n + 1)
                b2(n)
                if nx:
                    f3(n + 1)
                b3(n)
            s.barrier()
        if stop_after < 3:
            return
        with contextlib.ExitStack() as st:
            sg = Stager(k, KD, 256, st)
            wr = s.sb([128, KD, 512], BF16, "wr", st)
            sg.load(w_in_l[:, OFF_RG:OFF_RG + 256], wr, wr[:, :, 0:256])
            sg.load(w_in_l[:, OFF_RG + 256:OFF_RG + 512], wr, wr[:, :, 256:512])
            hc = [s.sb([128, KD, 512], BF16, "hc3", st) for _ in range(2)]
            psr = [s.ps([128, 512], F32, "psr", st) for _ in range(2)]
            psT = [s.ps([128, 8, 128], BF16, "psT", st) for _ in range(2)]
            sr = [s.sb([128, 512], BF16, "sr", st) for _ in range(2)]
            yo = [s.sb([128, 512], BF16, "yob", st) for _ in range(2)]
            i = 0
            s.dma(hc[0][:], hT_dram[:, :, 0:512], writes=[hc[0]], key="g3hc0")
            for tc in range(NCH):
                h = hc[tc % 2]
                tk = slice(tc * 512, (tc + 1) * 512)
                if tc + 1 < NCH:
                    s.dma(hc[(tc + 1) % 2][:], hT_dram[:, :, (tc + 1) * 512:(tc + 2) * 512], writes=[hc[(tc + 1) % 2]], key=f"g3hc{(tc + 1) % 2}")
                for m in range(4):
                    ps = psr[i % 2]
                    for kk in range(KD):
                        k.mm(ps[:], wr[:, kk, m * 128:(m + 1) * 128], h[:, kk, :], kk == 0, kk == KD - 1, [wr, h], [ps])
                    k.act(sr[i % 2][:], ps[:], AF.Silu, [ps], [sr[i % 2]])
                    pt = psT[i % 2]
                    for t in range(4):
                        k.tr(pt[:, t, :], o_n[:, tc * 4 + t, m * 128:(m + 1) * 128], c["ident"][:], [o_n, c["ident"]], [pt])
                    k.stt("dve", yo[i % 2][:], pt[:, 0:4, :].rearrange("p a b -> p (a b)"), gout_col, sr[i % 2][:], ALU.mult, ALU.mult,
                          [pt, sr[i % 2]], [yo[i % 2]])
                    s.dma(ybT_dram[:, m, tk], yo[i % 2][:], reads=[yo[i % 2]], key=f"ybo{i % 2}", eng="act")
                    i += 1
            s.barrier()


ATT_GROUPS = ((128, 1), (512, 4), (2048, 16))


def att_batches(dil):
    out = []
    if dil == 1:
        for b in range(8):
            out.append(([(0, 4 * b + q) for q in range(4)], ("flat", 0, 512 * b)))
    elif dil == 4:
        for r in range(4):
            for half in range(2):
                out.append(([(r, 4 * half + q) for q in range(4)], ("flat", r, 512 * half)))
    else:
        for b in range(8):
            out.append(([(2 * b, 0), (2 * b, 1), (2 * b + 1, 0), (2 * b + 1, 1)], ("pair", 2 * b, 0)))
    return out


def phase_attn(k, c, hT_dram, w_in_l, ycT_dram, hh_list=(0, 1, 2, 3)):
    s = k.s
    Stager._id = 0
    WStream._id = 0
    with contextlib.ExitStack() as st0:
        hT = s.sb([128, KD, S], BF16, "hTr", st0)
        hbufs = [s.buf("hTb") for _ in range(NCH)]
        for tc in range(NCH):
            s.dma(hT[:, :, tc * 512:(tc + 1) * 512], hT_dram[:, :, tc * 512:(tc + 1) * 512], writes=[hbufs[tc]], key=f"ahT{tc}")
        qT = [s.sb([128, S], BF16, "aqT", st0) for _ in range(2)]
        kT = [s.sb([128, S], BF16, "akT", st0) for _ in range(2)]
        vpm = [s.sb([128, 32, 128], BF16, "vpm", st0) for _ in range(2)]
        wqkv = [s.sb([128, KD, 384], BF16, "wqkv", st0) for _ in range(2)]
        sg = Stager(k, KD, 128, st0, nbuf=3, cast_eng=("pool", "pool", "dve"))
        nums = [s.sb([128, S], F32, "num", st0) for _ in range(2)]
        dens = [s.sb([128, S], F32, "den", st0) for _ in range(2)]
        Wc = [s.sb([128, 128], BF16, "Wc", st0) for _ in range(2)]
        Wp = [s.sb([128, 128], BF16, "Wp", st0) for _ in range(2)]
        Wt = s.sb([128, 128], F32, "Wt", st0)
        di = s.sb([128, 128], I32, "di", st0)
        dcur = s.sb([128, 128], F32, "dcur", st0)
        dprev = s.sb([128, 128], F32, "dprev", st0)
        s.op("pool", lambda e: e.iota(di[:], pattern=[[1, 128]], base=0, channel_multiplier=-1), [], [di])
        k.copy("pool", dcur[:], di[:], [di], [dcur])
        k.ts("dve", dprev[:], dcur[:], 128.0, None, ALU.add, None, [dcur], [dprev])
        k.ts("dve", dcur[:], dcur[:], 0.0, None, ALU.max, None, [dcur], [dcur])
        Ec = [s.sb([128, 4, 128], BF16, "Ec", st0)] * 2
        Ep = [s.sb([128, 4, 128], BF16, "Ep", st0)] * 2
        Pc = [s.sb([128, 4, 128], BF16, "Pc", st0) for _ in range(2)]
        Pp = [s.sb([128, 4, 128], BF16, "Pp", st0) for _ in range(2)]
        yo = [s.sb([128, 512], BF16, "ayo", st0) for _ in range(2)]
        pj = [s.ps([128, 512], F32, "pj", st0) for _ in range(2)]
        psSc = [s.ps([128, 4, 128], F32, "psSc", st0) for _ in range(2)]
        psSp = [s.ps([128, 4, 128], F32, "psSp", st0) for _ in range(2)]
        psN = s.ps([128, 4, 128], F32, "psN", st0)
        psD = s.ps([128, 4, 128], F32, "psD", st0)
        sc = 1.0 / math.sqrt(128.0)
        hcount = 0
        pji = 0
        bcount = 0
        heads = [(hh, g) for hh in hh_list for g in range(3)]
        pending = []

        def ldh(idx):
            hh_, g_ = heads[idx]
            hd_ = g_ * 4 + hh_
            w_ = wqkv[idx % 2]
            sg.load(w_in_l[:, OFF_QA + hd_ * 128: OFF_QA + (hd_ + 1) * 128], w_, w_[:, :, 0:128])
            sg.load(w_in_l[:, OFF_KA + hd_ * 128: OFF_KA + (hd_ + 1) * 128], w_, w_[:, :, 128:256])
            sg.load(w_in_l[:, OFF_VA + hd_ * 128: OFF_VA + (hd_ + 1) * 128], w_, w_[:, :, 256:384])
        ldh(0)
        for hi_, hh in enumerate(hh_list):
            num, den = nums[hi_ % 2], dens[hi_ % 2]
            for g, (window, dil) in enumerate(ATT_GROUPS):
                hd = g * 4 + hh
                i2 = hcount % 2
                hcount += 1
                w = wqkv[i2]
                if hcount < len(heads):
                    ldh(hcount)
                slope = 2.0 ** (-8.0 * (hd + 1) / 12.0)
                cf = slope * dil
                k.act(Wt[:], dcur[:], AF.Exp, [dcur], [Wt], scale=-cf)
                s.op("pool", lambda e, o=Wc[i2]: e.affine_select(out=o[:], in_=Wt[:], pattern=[[1, 128]], compare_op=ALU.is_ge,
                                                               fill=0.0, base=0, channel_multiplier=-1), [Wt], [Wc[i2]])
                k.act(Wt[:], dprev[:], AF.Exp, [dprev], [Wt], scale=-cf)
                s.op("pool", lambda e, o=Wp[i2]: e.affine_select(out=o[:], in_=Wt[:], pattern=[[-1, 128]], compare_op=ALU.is_ge,
                                                               fill=0.0, base=0, channel_multiplier=1), [Wt], [Wp[i2]])
                q_, k_, v_ = qT[i2], kT[i2], vpm[i2]
                for tc in range(NCH):
                    tk = slice(tc * 512, (tc + 1) * 512)
                    for (off, dst) in ((0, q_), (128, k_)):
                        ps = pj[pji % 2]
                        pji += 1
                        for kk in range(KD):
                            k.mm(ps[:], w[:, kk, off:off + 128], hT[:, kk, tk], kk == 0, kk == KD - 1, [w, hbufs[tc]], [ps])
                        k.copy(k.alt(), dst[:, tk], ps[:], [ps], [dst])
                nper = 32 // dil
                batches = att_batches(dil)

                def tsl(r, n):
                    st_ = r + 128 * dil * n
                    return slice(st_, st_ + 127 * dil + 1, dil)
                for blocks, _ in batches:
                    ps = pj[pji % 2]
                    pji += 1
                    for qi, (r, n) in enumerate(blocks):
                        for kk in range(KD):
                            k.mm(ps[:, qi * 128:(qi + 1) * 128], hT[:, kk, tsl(r, n)], w[:, kk, 256:384],
                                 kk == 0, kk == KD - 1, [w] + hbufs, [ps])
                    b0 = blocks[0][0] * nper + blocks[0][1]
                    if dil == 16:
                        b0 = blocks[0][0] * nper
                    k.copy(k.alt(), v_[:, b0:b0 + 4, :], ps[:].rearrange("p (a b) -> p a b", b=128), [ps], [v_])
                for blocks, (kind, r0, j0) in batches:
                    if pending:
                        pending.pop(0)()
                    bb = bcount % 2
                    bcount += 1
                    pc, pp = psSc[bb], psSp[bb]
                    has_prev = [n > 0 for (_, n) in blocks]
                    for qi, (r, n) in enumerate(blocks):
                        k.mm(pc[:, qi, :], k_[:, tsl(r, n)], q_[:, tsl(r, n)], True, True, [k_, q_], [pc])
                    for qi, (r, n) in enumerate(blocks):
                        if n > 0:
                            k.mm(pp[:, qi, :], k_[:, tsl(r, n - 1)], q_[:, tsl(r, n)], True, True, [k_, q_], [pp])
                    k.act(Ec[bb][:], pc[:], AF.Exp, [pc], [Ec[bb]], scale=sc)
                    k.tt("dve", Pc[bb][:], Ec[bb][:], Wc[i2][:].unsqueeze(1).to_broadcast([128, 4, 128]), ALU.mult,
                         [Ec[bb], Wc[i2]], [Pc[bb]])
                    pq = [qi for qi in range(4) if has_prev[qi]]
                    if pq:
                        q0, q1 = pq[0], pq[-1] + 1
                        if pq != list(range(q0, q1)):
                            for qi in pq:
                                k.act(Ep[bb][:, qi, :], pp[:, qi, :], AF.Exp, [pp], [Ep[bb]], scale=sc)
                                k.tt("dve", Pp[bb][:, qi, :], Ep[bb][:, qi, :], Wp[i2][:], ALU.mult, [Ep[bb], Wp[i2]], [Pp[bb]])
                        else:
                            k.act(Ep[bb][:, q0:q1, :], pp[:, q0:q1, :], AF.Exp, [pp], [Ep[bb]], scale=sc)
                            k.tt("dve", Pp[bb][:, q0:q1, :], Ep[bb][:, q0:q1, :],
                                 Wp[i2][:].unsqueeze(1).to_broadcast([128, q1 - q0, 128]), ALU.mult, [Ep[bb], Wp[i2]], [Pp[bb]])
                    for qi, (r, n) in enumerate(blocks):
                        bi = r * nper + n
                        k.mm(psN[:, qi, :], v_[:, bi, :], Pc[bb][:, qi, :], True, not has_prev[qi], [v_, Pc[bb]], [psN])
                        if has_prev[qi]:
                            k.mm(psN[:, qi, :], v_[:, bi - 1, :], Pp[bb][:, qi, :], False, True, [v_, Pp[bb]], [psN])
                    for qi, (r, n) in enumerate(blocks):
                        k.mm(psD[:, qi, :], c["ones"][:], Pc[bb][:, qi, :], True, not has_prev[qi], [c["ones"], Pc[bb]], [psD])
                        if has_prev[qi]:
                            k.mm(psD[:, qi, :], c["ones"][:], Pp[bb][:, qi, :], False, True, [c["ones"], Pp[bb]], [psD])
                    for acc, psx in ((num, psN), (den, psD)):
                        a3 = acc[:].rearrange("p (j r) -> p r j", r=dil)
                        if kind == "flat":
                            av = a3[:, r0, j0:j0 + 512]
                            pv = psx[:].rearrange("p a b -> p (a b)")
                        else:
                            av = a3[:, r0:r0 + 2, 0:256]
                            pv = psx[:].rearrange("p (a b) i -> p a (b i)", a=2)
                        if g == 0:
                            k.copy("dve", av, pv, [psx], [acc])
                        else:
                            k.tt("dve", av, pv, av, ALU.add, [psx, acc], [acc])
            def mk_fin(tc, hh=hh, num=num, den=den):
                def f():
                    tk = slice(tc * 512, (tc + 1) * 512)
                    k.act(den[:, tk], den[:, tk], AF.Ln, [den], [den])
                    k.act(den[:, tk], den[:, tk], AF.Exp, [den], [den], scale=-1.0)
                    y = yo[tc % 2]
                    k.tt("pool", y[:], num[:, tk], den[:, tk], ALU.mult, [num, den], [y])
                    s.dma(ycT_dram[:, hh, tk], y[:], reads=[y], key=f"ayo{tc % 2}", eng="sp")
                return f
            for tc in range(NCH):
                pending.append(mk_fin(tc))
        while pending:
            pending.pop(0)()
        s.barrier()


def phase_merge(k, c, hT_dram, yT_drams, w_in_l, w_branch_l, w_out_l, x_src, x_dst):
    s = k.s
    Stager._id = 0
    WStream._id = 0
    HALF = S // 2
    with contextlib.ExitStack() as st0:
        wo = s.sb([128, KD, D], BF16, "wo", st0)
        sgo = Stager(k, KD, 256, st0)
        hT = s.sb([128, KD, HALF], BF16, "mhT", st0)
        yT = [s.sb([128, 4, HALF], BF16, "myT", st0) for _ in range(3)]
        mT = s.sb([128, KD, HALF], BF16, "mT", st0)
        wg = [[s.sb([128, KD, 128], BF16, "wg", st0) for _ in range(3)] for _ in range(2)]
        wb = [[s.sb([128, 4, 128], BF16, "wb", st0) for _ in range(3)] for _ in range(2)]
        sg = Stager(k, KD, 128, st0, nbuf=3, cast_eng=("dve", "act"))
        sig = [s.sb([128, 512], BF16, "sig", st0) for _ in range(3)]
        acc = [s.sb([128, 512], F32, "macc", st0) for _ in range(2)]
        tmp = [s.sb([128, 512], F32, "mtmp", st0) for _ in range(2)]
        xt = [s.sb([128, D], F32, "mxt", st0) for _ in range(2)]
        psG = [s.ps([128, 512], F32, "psG", st0) for _ in range(3)]
        psP = [s.ps([128, 512], F32, "psP", st0) for _ in range(3)]
        psX = [s.ps([128, 512], F32, "psX", st0) for _ in range(2)]
        gi = 0
        hb = [s.buf("mhb") for _ in range(4)]
        yb = [[s.buf("myb") for _ in range(4)] for _ in range(3)]
        mbufs = [s.buf("mTb") for _ in range(KD)]
        for half in range(2):
            t0 = half * HALF
            for q in range(4):
                sl = slice(q * 512, (q + 1) * 512)
                gl = slice(t0 + q * 512, t0 + (q + 1) * 512)
                s.dma(hT[:, :, sl], hT_dram[:, :, gl], writes=[hb[q]], key=f"mh{q}")
                for n in range(3):
                    s.dma(yT[n][:, :, sl], yT_drams[n][:, :, gl], writes=[yb[n][q]], key=f"my{n}{q}")
            def ldw(m_, ws_):
                for n in range(3):
                    c0 = OFF_GT + n * D + m_ * 128
                    sg.load(w_in_l[:, c0:c0 + 128], wg[ws_][n], wg[ws_][n][:])
                    sg.load(w_branch_l[n, :, m_ * 128:(m_ + 1) * 128], wb[ws_][n], wb[ws_][n][:])
            if half == 0:
                ldw(0, gi % 2)
                for j in range(4):
                    sgo.load(w_out_l[:, j * 256:(j + 1) * 256], wo, wo[:, :, j * 256:(j + 1) * 256])
            for m in range(KD):
                ws = gi % 2
                gi += 1
                if m + 1 < KD:
                    ldw(m + 1, gi % 2)
                elif half == 0:
                    ldw(0, gi % 2)
                for q in range(4):
                    sl = slice(q * 512, (q + 1) * 512)
                    a = acc[q % 2]
                    for n in range(3):
                        pg, pp = psG[n], psP[n]
                        for kk in range(KD):
                            k.mm(pg[:], wg[ws][n][:, kk, :], hT[:, kk, sl], kk == 0, kk == KD - 1, [wg[ws][n], hb[q]], [pg])
                        for kk in range(4):
                            k.mm(pp[:], wb[ws][n][:, kk, :], yT[n][:, kk, sl], kk == 0, kk == 3, [wb[ws][n], yb[n][q]], [pp])
                        k.act(sig[n][:], pg[:], AF.Sigmoid, [pg], [sig[n]])
                        if n == 0:
                            k.tt("dve", a[:], pp[:], sig[n][:], ALU.mult, [pp, sig[n]], [a])
                        elif n == 1:
                            tq = tmp[0]
                            k.tt("dve", tq[:], pp[:], sig[n][:], ALU.mult, [pp, sig[n]], [tq])
                            k.tt("dve", a[:], a[:], tq[:], ALU.add, [a, tq], [a])
                        else:
                            tq = tmp[1]
                            k.tt("dve", tq[:], pp[:], sig[n][:], ALU.mult, [pp, sig[n]], [tq])
                            k.tt("pool", mT[:, m, sl], a[:], tq[:], ALU.add, [a, tq], [mbufs[m]])
            for t in range(HALF // 128):
                T = t0 // 128 + t
                x_t = xt[t % 2]
                s.dma(x_t[:], x_src[T * 128:(T + 1) * 128, :], writes=[x_t], key=f"mx{t % 2}")
                for hf in range(2):
                    ps = psX[hf]
                    for kk in range(KD):
                        k.mm(ps[:], mT[:, kk, t * 128:(t + 1) * 128], wo[:, kk, hf * 512:(hf + 1) * 512],
                             kk == 0, kk == KD - 1, [mbufs[kk], wo], [ps])
                    k.tt("dve", x_t[:, hf * 512:(hf + 1) * 512], ps[:], x_t[:, hf * 512:(hf + 1) * 512], ALU.add, [ps, x_t], [x_t])
                s.dma(x_dst[T * 128:(T + 1) * 128, :], x_t[:], reads=[x_t], key=f"mxo{t % 2}", eng="act")
        s.barrier()


def phase_memnorm(k, c, mem_dram, gcol, memT):
    s = k.s
    with contextlib.ExitStack() as st:
        xb = [s.sb([128, D], F32, "mxb", st) for _ in range(2)]
        junk = s.sb([128, D], BF16, "mjunk", st)
        xs = [s.sb([128, D], BF16, "mxs", st) for _ in range(2)]
        ss = s.sb([128, 2], F32, "mss", st)
        pst = [s.ps([128, KD, 128], BF16, "mpst", st) for _ in range(2)]
        k.memset("pool", ss[:], 0.0, [ss])
        for t in range(2):
            s.dma(xb[t][:], mem_dram[t * 128:(t + 1) * 128, :], writes=[xb[t]], key=f"mm{t}")
            k.act(junk[:], xb[t][:], AF.Square, [xb[t]], [junk, ss], accum_out=ss[:, t:t + 1])
        k.act(ss[:], ss[:], AF.Sqrt, [ss], [ss], bias=1e-6, scale=1.0 / D)
        s.op("dve", lambda e: e.reciprocal(out=ss[:], in_=ss[:]), [ss], [ss])
        for t in range(2):
            k.ts("dve", xs[t][:], xb[t][:], ss[:, t:t + 1], None, ALU.mult, None, [xb[t], ss], [xs[t]])
            for kk in range(KD):
                k.tr(pst[t][:, kk, :], xs[t][:, kk * 128:(kk + 1) * 128], c["ident"][:], [xs[t], c["ident"]], [pst[t]])
            k.tt("dve", memT[:, :, t * 128:(t + 1) * 128], pst[t][:],
                 gcol.unsqueeze(2).to_broadcast([128, KD, 128]), ALU.mult, [pst[t]], [memT])
        s.barrier()


def phase_cross(k, c, hT_dram, memT, w_xq_l, w_xkv_l, w_xo_l, x_src, x_dst, after_loads=None):
    s = k.s
    Stager._id = 0
    WStream._id = 0
    with contextlib.ExitStack() as st0:
        wq = s.sb([128, KD, D], BF16, "xwq", st0)
        wo = s.sb([128, KD, D], BF16, "xwo", st0)
        KT = s.sb([128, KD, MEM], BF16, "xKT", st0)
        Vm = s.sb([128, 2, D], BF16, "xVm", st0)
        with contextlib.ExitStack() as st:
            sg = Stager(k, KD, 256, st, cast_eng=("dve", "act", "pool"))
            wkv = [s.sb([128, KD, 256], BF16, "wkv", st) for _ in range(2)]
            pk = [s.ps([128, 512], F32, "xpk", st) for _ in range(2)]
            pi = 0
            for j in range(8):
                w = wkv[j % 2]
                sg.load(w_xkv_l[:, j * 256:(j + 1) * 256], w, w[:])
                if j < 4:
                    for mm_ in range(2):
                        ps = pk[pi % 2]
                        pi += 1
                        for kk in range(KD):
                            k.mm(ps[:, 0:MEM], w[:, kk, mm_ * 128:(mm_ + 1) * 128], memT[:, kk, :], kk == 0, kk == KD - 1, [w, memT], [ps])
                        k.copy(k.alt(), KT[:, j * 2 + mm_, :], ps[:, 0:MEM], [ps], [KT])
                else:
                    for t in range(2):
                        ps = pk[pi % 2]
                        pi += 1
                        for kk in range(KD):
                            k.mm(ps[:, 0:256], memT[:, kk, t * 128:(t + 1) * 128], w[:, kk, :], kk == 0, kk == KD - 1, [w, memT], [ps])
                        k.copy(k.alt(), Vm[:, t, (j - 4) * 256:(j - 3) * 256], ps[:, 0:256], [ps], [Vm])
            for j in range(4):
                sg.load(w_xq_l[:, j * 256:(j + 1) * 256], wq, wq[:, :, j * 256:(j + 1) * 256])
            for j in range(4):
                sg.load(w_xo_l[:, j * 256:(j + 1) * 256], wo, wo[:, :, j * 256:(j + 1) * 256])
            s.barrier()
        if after_loads is not None:
            after_loads()
        hc = [s.sb([128, KD, 512], BF16, "xhc", st0) for _ in range(2)]
        qT = [s.sb([128, KD, 512], BF16, "xqT", st0) for _ in range(2)]
        oT = [s.sb([128, KD, 512], BF16, "xoT", st0) for _ in range(2)]
        E = [s.sb([128, 2, 512], BF16, "xE", st0) for _ in range(2)]
        rden = [s.sb([128, 512], F32, "xrd", st0) for _ in range(2)]
        xt = [s.sb([128, D], F32, "xxt", st0) for _ in range(2)]
        psq = [s.ps([128, 512], F32, "xpsq", st0) for _ in range(2)]
        psS = [s.ps([128, 512], F32, "xpsS", st0) for _ in range(2)]
        psO = [s.ps([128, 512], F32, "xpsO", st0) for _ in range(2)]
        psD = s.ps([128, 512], F32, "xpsD", st0)
        psX = s.ps([128, 512], F32, "xpsX", st0)
        qi = 0
        ei = 0
        xi = 0
        s.dma(hc[0][:], hT_dram[:, :, 0:512], writes=[hc[0]], key="xh0")
        for tc in range(NCH):
            h = hc[tc % 2]
            q_ = qT[tc % 2]
            o_ = oT[tc % 2]
            tk = slice(tc * 512, (tc + 1) * 512)
            if tc + 1 < NCH:
                s.dma(hc[(tc + 1) % 2][:], hT_dram[:, :, (tc + 1) * 512:(tc + 2) * 512], writes=[hc[(tc + 1) % 2]], key=f"xh{(tc + 1) % 2}")
            for m in range(KD):
                ps = psq[qi % 2]
                qi += 1
                for kk in range(KD):
                    k.mm(ps[:], wq[:, kk, m * 128:(m + 1) * 128], h[:, kk, :], kk == 0, kk == KD - 1, [wq, h], [ps])
                k.copy(k.alt(), q_[:, m, :], ps[:], [ps], [q_])
            for hd in range(4):
                e_ = E[ei % 2]
                rd = rden[ei % 2]
                ei += 1
                for jt in range(2):
                    ps = psS[jt]
                    for hf in range(2):
                        k.mm(ps[:], KT[:, hd * 2 + hf, jt * 128:(jt + 1) * 128], q_[:, hd * 2 + hf, :], hf == 0, hf == 1, [KT, q_], [ps])
                    k.act(e_[:, jt, :], ps[:], AF.Exp, [ps], [e_], scale=1.0 / 16.0)
                for jt in range(2):
                    k.mm(psD[:], c["ones"][:], e_[:, jt, :], jt == 0, jt == 1, [c["ones"], e_], [psD])
                k.act(rd[:], psD[:], AF.Ln, [psD], [rd])
                k.act(rd[:], rd[:], AF.Exp, [rd], [rd], scale=-1.0)
                for cc in range(2):
                    ps = psO[cc]
                    for jt in range(2):
                        k.mm(ps[:], Vm[:, jt, hd * 256 + cc * 128: hd * 256 + (cc + 1) * 128], e_[:, jt, :], jt == 0, jt == 1, [Vm, e_], [ps])
                    k.tt("dve", o_[:, hd * 2 + cc, :], ps[:], rd[:], ALU.mult, [ps, rd], [o_])
            for t in range(4):
                T = tc * 4 + t
                x_t = xt[xi % 2]
                xi += 1
                s.dma(x_t[:], x_src[T * 128:(T + 1) * 128, :], writes=[x_t], key=f"xx{xi % 2}")
                for hf in range(2):
                    for kk in range(KD):
                        k.mm(psX[:], o_[:, kk, t * 128:(t + 1) * 128], wo[:, kk, hf * 512:(hf + 1) * 512],
                             kk == 0, kk == KD - 1, [o_, wo], [psX])
                    k.tt("dve", x_t[:, hf * 512:(hf + 1) * 512], psX[:], x_t[:, hf * 512:(hf + 1) * 512], ALU.add, [psX, x_t], [x_t])
                s.dma(x_dst[T * 128:(T + 1) * 128, :], x_t[:], reads=[x_t], key=f"xxo{xi % 2}", eng="act")
        s.barrier()


def mlp_prefetch_up(k, st, w_up_l):
    s = k.s
    wu = s.sb([128, KD, 4 * D], BF16, "wu", st)
    ub = [[s.buf("wub") for _ in range(KD)] for _ in range(4)]

    def issue():
        for qd in range(4):
            for kk in range(KD):
                s.wload(wu[:, kk, qd * 1024:(qd + 1) * 1024], w_up_l[kk * 128:(kk + 1) * 128, qd * 1024:(qd + 1) * 1024],
                        [ub[qd][kk]], partial=False)
    return wu, ub, issue


def phase_mlp(k, c, hT_dram, w_up_l, w_down_l, x_src, x_dst, pre=None):
    s = k.s
    Stager._id = 0
    WStream._id = 0
    NF = 32
    with contextlib.ExitStack() as st0:
        if pre is None:
            wu, ub, issue = mlp_prefetch_up(k, st0, w_up_l)
            issue()
        else:
            wu, ub = pre
        wd = s.sb([128, NF, D], BF16, "wd", st0)
        db = [s.buf("wdb") for _ in range(NF)]
        for j in range(NF):
            s.wload(wd[:, j, :], w_down_l[j * 128:(j + 1) * 128, :], [db[j]], partial=False)
        hc = [s.sb([128, KD, 512], BF16, "ghc", st0) for _ in range(2)]
        rl = [s.sb([128, 512], BF16, "grl", st0) for _ in range(2)]
        aT = s.sb([128, NF, 512], BF16, "aT", st0)
        ab = [s.buf("aTb") for _ in range(NF)]
        xt = [s.sb([128, D], F32, "gxt", st0) for _ in range(2)]
        psU = [s.ps([128, 512], F32, "psU", st0) for _ in range(3)]
        psDn = [s.ps([128, 512], F32, "psDn", st0) for _ in range(2)]
        ui = 0
        xi = 0
        s.dma(hc[0][:], hT_dram[:, :, 0:512], writes=[hc[0]], key="gh0")
        for tc in range(NCH):
            h = hc[tc % 2]
            tk = slice(tc * 512, (tc + 1) * 512)
            if tc + 1 < NCH:
                s.dma(hc[(tc + 1) % 2][:], hT_dram[:, :, (tc + 1) * 512:(tc + 2) * 512], writes=[hc[(tc + 1) % 2]], key=f"gh{(tc + 1) % 2}")
            for f in range(NF):
                ps = psU[ui % 3]
                r_ = rl[ui % 2]
                ui += 1
                for kk in range(KD):
                    k.mm(ps[:], wu[:, kk, f * 128:(f + 1) * 128], h[:, kk, :], kk == 0, kk == KD - 1, [ub[f // 8][kk], h], [ps])
                k.act(r_[:], ps[:], AF.Relu, [ps], [r_])
                k.tt("pool", aT[:, f, :], r_[:], r_[:], ALU.mult, [r_], [ab[f]])
            for t in range(4):
                T = tc * 4 + t
                x_t = xt[xi % 2]
                xi += 1
                s.dma(x_t[:], x_src[T * 128:(T + 1) * 128, :], writes=[x_t], key=f"gx{xi % 2}")
                for hf in range(2):
                    ps = psDn[hf]
                    for f in range(NF):
                        k.mm(ps[:], aT[:, f, t * 128:(t + 1) * 128], wd[:, f, hf * 512:(hf + 1) * 512],
                             f == 0, f == NF - 1, [ab[f], db[f]], [ps])
                    k.tt("dve", x_t[:, hf * 512:(hf + 1) * 512], ps[:], x_t[:, hf * 512:(hf + 1) * 512], ALU.add, [ps, x_t], [x_t])
                s.dma(x_dst[T * 128:(T + 1) * 128, :], x_t[:], reads=[x_t], key=f"gxo{xi % 2}", eng="act")
        s.barrier()


def phase_final(k, c, x_src, gfull_dram, out_dram):
    s = k.s
    with contextlib.ExitStack() as st:
        gb = s.sb([128, D], F32, "gfin", st)
        s.dma(gb[:], gfull_dram, writes=[gb], key="gfin")
        xb = [s.sb([128, D], F32, "fxb", st) for _ in range(3)]
        junk = s.sb([128, D], BF16, "fjunk", st)
        yo = [s.sb([128, D], F32, "fyo", st) for _ in range(2)]
        ss = s.sb([128, NT], F32, "fss", st)
        ssb = [s.buf("fssb") for _ in range(NT)]
        k.memset("pool", ss[:], 0.0, ssb)
        outs = []

        def fa(t):
            x_t = xb[t % 3]
            s.dma(x_t[:], x_src[t * 128:(t + 1) * 128, :], writes=[x_t], key=f"fx{t % 3}")
            k.act(junk[:], x_t[:], AF.Square, [x_t], [junk, ssb[t]], accum_out=ss[:, t:t + 1])
            k.act(ss[:, t:t + 1], ss[:, t:t + 1], AF.Sqrt, [ssb[t]], [ssb[t]], bias=1e-6, scale=1.0 / D)
            s.op("dve", lambda e, t=t: e.reciprocal(out=ss[:, t:t + 1], in_=ss[:, t:t + 1]), [ssb[t]], [ssb[t]])

        def fb(t):
            x_t = xb[t % 3]
            y = yo[t % 2]
            k.stt("dve", y[:], x_t[:], ss[:, t:t + 1], gb[:], ALU.mult, ALU.mult, [x_t, ssb[t], gb], [y])
            outs.append(s.dma(out_dram[t * 128:(t + 1) * 128, :], y[:], reads=[y], key=f"fo{t % 2}", eng="act"))
        fa(0)
        fa(1)
        for t in range(NT):
            if t + 2 < NT:
                fa(t + 2)
            fb(t)
        s.barrier()
    return outs


def prep_s5(a_re, a_im, log_step, b_re, b_im, c_re, c_im, d_skip):
    f = np.float32
    bre = np.zeros((128, 16, 64), f)
    bim = np.zeros((128, 16, 64), f)
    ar = np.zeros((128, 16, 64), f)
    ai = np.zeros((128, 16, 64), f)
    ls = np.zeros((128, 16, 64), f)
    dm = np.zeros((128, 16, 64), f)
    for g in range(32):
        kc, hb, j = g // 8, (g % 8) // 4, g % 4
        idx = kc * 4 + j
        rows = slice(64 * hb + 16 * j, 64 * hb + 16 * j + 16)
        bre[rows, idx, :] = b_re[g].T
        bim[rows, idx, :] = b_im[g].T
        blk = slice(64 * hb, 64 * hb + 64)
        ar[blk, idx, :] = a_re[g][None, :]
        ai[blk, idx, :] = a_im[g][None, :]
        ls[blk, idx, :] = log_step[g]
        gp = g // 2
        off = 16 * (g % 2) + (32 if gp % 4 == 3 else 0)
        for cc in range(16):
            dm[64 * hb + 16 * j + cc, gp, off + cc] = d_skip[g * 16 + cc]
    st = np.zeros((128, 3, 32), f)
    st[0:64, 0, :] = a_re.T
    st[64:128, 0, :] = a_re.T
    st[0:64, 1, :] = a_im.T
    st[64:128, 1, :] = a_im.T
    st[:, 2, :] = log_step[None, :]
    c1 = np.zeros((128, 32, 16), f)
    c2 = np.zeros((128, 32, 16), f)
    c1[0:64] = c_re.transpose(2, 0, 1)
    c1[64:128] = c_im.transpose(2, 0, 1)
    c2[0:64] = c_im.transpose(2, 0, 1)
    c2[64:128] = c_re.transpose(2, 0, 1)
    return dict(bre=bre, bim=bim, ar=ar, ai=ai, ls=ls, st=st, c1=c1, c2=c2, dm=dm)


def col_layout(v):
    return np.ascontiguousarray(np.asarray(v, np.float32).reshape(-1, 128).T)


S5_SHAPES = [("bre", [128, 16, 64]), ("bim", [128, 16, 64]), ("ar", [128, 16, 64]), ("ai", [128, 16, 64]),
             ("ls", [128, 16, 64]), ("st", [128, 3, 32]), ("c1", [128, 32, 16]), ("c2", [128, 32, 16]), ("dm", [128, 16, 64])]
NVEC = 80


def build_program(n_layers=DEPTH, dbg=False):
    nc = bass.Bass("TRN2", target_bir_lowering=False)

    def din(name, shape, dt=F32):
        return nc.dram_tensor(name, list(shape), dt, kind="ExternalInput").ap()

    def dscr(name, shape, dt):
        if dbg:
            return nc.dram_tensor(name, list(shape), dt, kind="ExternalOutput").ap()
        return nc.dram_tensor(name, list(shape), dt).ap()

    x = din("x", [S, D])
    mem = din("mem", [MEM, D])
    vec = din("vec", [128, NVEC])
    gfin = din("gfin", [128, D])
    w_in = din("w_in", [DEPTH, D, D_IN])
    w_glu = din("w_glu", [DEPTH, 512, 512])
    w_gate = din("w_gla_gate", [DEPTH, 16, 256])
    b_gate = din("b_gla_gate", [DEPTH, 1, 256])
    w_branch = din("w_branch", [DEPTH, 3, 512, D])
    w_out = din("w_out", [DEPTH, D, D])
    w_xq = din("w_xq", [DEPTH, D, D])
    w_xkv = din("w_xkv", [DEPTH, D, 2 * D])
    w_xo = din("w_xo", [DEPTH, D, D])
    w_up = din("w_up", [DEPTH, D, 4 * D])
    w_down = din("w_down", [DEPTH, 4 * D, D])
    p5 = {nm: din("s5_" + nm, [DEPTH] + shp) for nm, shp in S5_SHAPES}
    out = nc.dram_tensor("out", [S, D], F32, kind="ExternalOutput").ap()
    hT = dscr("hT_scr", [128, KD, S], BF16)
    yaT = dscr("yaT_scr", [128, 4, S], BF16)
    ybT = dscr("ybT_scr", [128, 4, S], BF16)
    ycT = dscr("ycT_scr", [128, 4, S], BF16)
    xr = dscr("xr_scr", [S, D], F32)

    k = K(nc)
    s = k.s
    c = make_consts(k)
    vs = s.sb([128, NVEC], F32, "vec")
    s.dma(vs[:], vec, writes=[vs], key="vec")
    memT = s.sb([128, KD, MEM], BF16, "memT")
    s.barrier()
    phase_memnorm(k, c, mem, vs[:, 64:72], memT)
    x_cur = x
    for l in range(n_layers):
        o = 32 * l
        phase_norm(k, c, x_cur, vs[:, o:o + 8], hT)
        phase_s5(k, c, hT, w_in[l], {nm: p5[nm][l] for nm, _ in S5_SHAPES}, w_glu[l], vs[:, o + 24:o + 28], yaT)
        phase_gla(k, c, hT, w_in[l], w_gate[l], b_gate[l], vs[:, o + 28:o + 29], ybT)
        phase_attn(k, c, hT, w_in[l], ycT)
        phase_merge(k, c, hT, [yaT, ybT, ycT], w_in[l], w_branch[l], w_out[l], x_cur, xr)
        x_cur = xr
        phase_norm(k, c, xr, vs[:, o + 8:o + 16], hT)
        with contextlib.ExitStack() as lst:
            wu, ub, issue = mlp_prefetch_up(k, lst, w_up[l])
            phase_cross(k, c, hT, memT, w_xq[l], w_xkv[l], w_xo[l], xr, xr, after_loads=issue)
            phase_norm(k, c, xr, vs[:, o + 16:o + 24], hT)
            phase_mlp(k, c, hT, w_up[l], w_down[l], xr, xr, pre=(wu, ub))
    outs = phase_final(k, c, x_cur, gfin, out)
    s.emit(final_wait_ops=outs)
    info = {"ops": {e: len(v) for e, v in s.ops.items()}, "waits": s.nwaits,
            "maxcount": max(s.counters.values()), "nsem": len(s.counters)}
    s.close()
    return nc, info


def host_inputs(inp, b):
    f = np.float32
    vec = np.zeros((128, NVEC), f)
    for l in range(DEPTH):
        o = 32 * l
        vec[:, o:o + 8] = col_layout(inp["g_mix"][l])
        vec[:, o + 8:o + 16] = col_layout(inp["g_cross"][l])
        vec[:, o + 16:o + 24] = col_layout(inp["g_mlp"][l])
        vec[:, o + 24:o + 28] = col_layout(inp["b_glu"][l])
        vec[:, o + 28:o + 29] = col_layout(inp["g_gla_out"][l])
    vec[:, 64:72] = col_layout(inp["g_mem"])
    m = {
        "x": np.ascontiguousarray(inp["x"][b], dtype=f),
        "mem": np.ascontiguousarray(inp["mem"][b], dtype=f),
        "vec": vec,
        "gfin": np.ascontiguousarray(np.broadcast_to(np.asarray(inp["g_final"], f)[None, :], (128, D))),
        "b_gla_gate": np.ascontiguousarray(np.asarray(inp["b_gla_gate"], f)[:, None, :]),
    }
    for nm in ("w_in", "w_glu", "w_gla_gate", "w_branch", "w_out", "w_xq", "w_xkv", "w_xo", "w_up", "w_down"):
        m[nm] = np.ascontiguousarray(inp[nm], dtype=f)
    per = [prep_s5(inp["s5_a_re"][l], inp["s5_a_im"][l], inp["s5_log_step"][l], inp["s5_b_re"][l], inp["s5_b_im"][l],
                   inp["s5_c_re"][l], inp["s5_c_im"][l], inp["s5_d"][l]) for l in range(DEPTH)]
    for nm, _ in S5_SHAPES:
        m["s5_" + nm] = np.stack([per[l][nm] for l in range(DEPTH)], axis=0)
    return m


_CACHE = {}


def kernel(**inputs):
    inp = {k_: np.asarray(v) for k_, v in inputs.items()}
    if "nc" not in _CACHE:
        _CACHE["nc"] = build_program()[0]
    nc = _CACHE["nc"]
    n = inp["x"].shape[0]
    shared = host_inputs(inp, 0)
    in_maps = []
    for b in range(n):
        m = dict(shared)
        m["x"] = np.ascontiguousarray(inp["x"][b], dtype=np.float32)
        m["mem"] = np.ascontiguousarray(inp["mem"][b], dtype=np.float32)
        in_maps.append(m)
    res = run_bass_kernel_spmd(nc, in_maps, core_ids=list(range(n)))
    return np.stack([np.asarray(r["out"]) for r in res.results], axis=0).astype(np.float32)
```

```python
import math
import contextlib
import numpy as np
import concourse.bass as bass
import concourse.mybir as mybir
from concourse.bass_utils import run_bass_kernel_spmd

F32 = mybir.dt.float32
BF16 = mybir.dt.bfloat16
I32 = mybir.dt.int32
AF = mybir.ActivationFunctionType
ALU = mybir.AluOpType

SAME_ENGINE_SYNC = True
NOSYNC_ENGINES = ("pe",)

S = 4096
D = 1024
NT = S // 128
NCH = S // 512
KD = D // 128
DEPTH = 2
MEM = 256
D_IN = 9744
OFF_U, OFF_QG, OFF_KG, OFF_VG, OFF_LR, OFF_RG, OFF_QA, OFF_KA, OFF_VA, OFF_GT = (
    0, 512, 768, 1024, 1536, 1552, 2064, 3600, 5136, 6672)
TWO_PI = 2.0 * math.pi
SIN_SCALE = 6.2831
S5E = 'dve,pool,pool,dve'.split(',')


class Buf:
    __slots__ = ("name", "writers", "readers", "war")

    def __init__(self, name):
        self.name = name
        self.writers = []
        self.readers = []
        self.war = []


class Op:
    __slots__ = ("eng", "fn", "deps", "ref", "val", "semkey", "is_dma")

    def __init__(self, eng, fn, is_dma=False, semkey=None):
        self.eng = eng
        self.fn = fn
        self.deps = []
        self.ref = False
        self.val = None
        self.semkey = semkey
        self.is_dma = is_dma


class Tile:
    def __init__(self, t, b):
        self.t = t
        self.b = b

    def __getitem__(self, k):
        return self.t[k]


class Sched:
    ENG = ("pe", "dve", "act", "pool", "sp")

    def __init__(self, nc):
        self.nc = nc
        self.ops = {e: [] for e in self.ENG}
        self.all_ops = []
        self.stack = contextlib.ExitStack()
        self._n = 0
        self.pending = {e: [] for e in self.ENG}
        self.last = {e: None for e in self.ENG}
        self.open_dmas = []
        self.last_dma = {}

    def sb(self, shape, dtype, name=None, stack=None):
        self._n += 1
        nm = f"{name or 'sb'}_{self._n}"
        t = (stack or self.stack).enter_context(self.nc.sbuf_tensor(nm, list(shape), dtype))
        return Tile(t, Buf(nm))

    def ps(self, shape, dtype, name=None, stack=None):
        self._n += 1
        nm = f"{name or 'ps'}_{self._n}"
        t = (stack or self.stack).enter_context(self.nc.psum_tensor(nm, list(shape), dtype))
        return Tile(t, Buf(nm))

    def buf(self, name=None):
        self._n += 1
        return Buf(f"{name or 'b'}_{self._n}")

    def op(self, eng, fn, reads=(), writes=(), dma_key=None, partial=False):
        is_dma = dma_key is not None
        o = Op(eng, fn, is_dma=is_dma, semkey=dma_key if is_dma else eng)
        reads = [r.b if isinstance(r, Tile) else r for r in reads]
        writes = [w.b if isinstance(w, Tile) else w for w in writes]
        deps = list(self.pending[eng])
        self.pending[eng] = []
        if is_dma:
            prev = self.last_dma.get(dma_key)
            if prev is not None:
                deps.append(prev)
        for b in reads:
            deps.extend(b.writers)
        for b in writes:
            if partial and not b.readers:
                deps.extend(b.war)
            else:
                deps.extend(b.writers)
                deps.extend(b.readers)
        seen = set()
        for d in deps:
            if d is None or id(d) in seen or d is o:
                continue
            seen.add(id(d))
            if (not d.is_dma) and (not is_dma) and d.eng == eng:
                if eng in NOSYNC_ENGINES or not SAME_ENGINE_SYNC:
                    continue
            o.deps.append(d)
            d.ref = True
        for b in writes:
            if partial and not b.readers:
                b.writers = b.writers + [o]
            else:
                b.war = list(b.readers) if partial else []
                b.writers = [o]
                b.readers = []
        for b in reads:
            if b not in writes:
                b.readers.append(o)
        self.ops[eng].append(o)
        self.all_ops.append(o)
        if is_dma:
            self.open_dmas.append(o)
            self.last_dma[dma_key] = o
        else:
            self.last[eng] = o
        return o

    def dma(self, out, in_, reads=(), writes=(), key="dma", eng="sp", partial=False, **kw):
        return self.op(eng, lambda e: e.dma_start(out=out, in_=in_, **kw), reads, writes, dma_key=key, partial=partial)

    def wload(self, out, in_, writes, partial=True):
        self._wl = getattr(self, "_wl", 0) + 1
        return self.dma(out, in_, writes=writes, key=f"wl{self._wl % 12}", eng="pool", partial=partial)

    def barrier(self):
        deps = [self.last[e] for e in self.ENG if self.last[e] is not None] + list(self.open_dmas)
        for e in self.ENG:
            self.pending[e] = list(deps) + self.pending[e]
        self.open_dmas = []

    def emit(self, final_wait_ops=()):
        nc = self.nc
        counters = {}
        fset = set(id(o) for o in final_wait_ops)
        for o in self.all_ops:
            if id(o) in fset or o.is_dma:
                o.ref = True
        for o in self.all_ops:
            if o.ref:
                step = 16 if o.is_dma else 1
                counters[o.semkey] = counters.get(o.semkey, 0) + step
                o.val = counters[o.semkey]
        self.counters = counters
        sems = {}
        for k in counters:
            sems[k] = self.stack.enter_context(nc.semaphore(f"s_{k}"))
        block = self.stack.enter_context(nc.Block())
        engmap = {"pe": block.tensor, "dve": block.vector, "act": block.scalar,
                  "pool": block.gpsimd, "sp": block.sync}
        stats = {"waits": 0}
        for ename in self.ENG:
            ops = self.ops[ename]
            if not ops and not (ename == "sp" and final_wait_ops):
                continue

            def body(e, ops=ops, ename=ename):
                seen = {}
                for o in ops:
                    need = {}
                    for d in o.deps:
                        if seen.get(d.semkey, 0) >= d.val:
                            continue
                        if need.get(d.semkey, 0) < d.val:
                            need[d.semkey] = d.val
                    for k, v in need.items():
                        e.wait_ge(sems[k], v)
                        seen[k] = v
                        stats["waits"] += 1
                    ins = o.fn(e)
                    if o.ref:
                        ins.then_inc(sems[o.semkey], 16 if o.is_dma else 1)
                if ename == "sp":
                    for o in final_wait_ops:
                        if seen.get(o.semkey, 0) < o.val:
                            e.wait_ge(sems[o.semkey], o.val)
                            seen[o.semkey] = o.val

            engmap[ename](body)
        self.nwaits = stats["waits"]

    def close(self):
        self.stack.close()


class K:
    def __init__(self, nc):
        self.nc = nc
        self.s = Sched(nc)
        self.rr = 0
        self.dbg_out = None

    def tt(self, eng, out, in0, in1, op, reads, writes):
        return self.s.op(eng, lambda e: e.tensor_tensor(out=out, in0=in0, in1=in1, op=op), reads, writes)

    def ts(self, eng, out, in0, s1, s2, op0, op1, reads, writes):
        if s2 is None:
            return self.s.op(eng, lambda e: e.tensor_scalar(out=out, in0=in0, scalar1=s1, scalar2=None, op0=op0), reads, writes)
        return self.s.op(eng, lambda e: e.tensor_scalar(out=out, in0=in0, scalar1=s1, scalar2=s2, op0=op0, op1=op1), reads, writes)

    def stt(self, eng, out, in0, scalar, in1, op0, op1, reads, writes):
        return self.s.op(eng, lambda e: e.scalar_tensor_tensor(out=out, in0=in0, scalar=scalar, in1=in1, op0=op0, op1=op1), reads, writes)

    def act(self, out, in_, func, reads, writes, bias=None, scale=None, accum_out=None):
        kw = {}
        if bias is not None:
            kw["bias"] = bias
        if scale is not None:
            kw["scale"] = scale
        if accum_out is not None:
            kw["accum_out"] = accum_out
        return self.s.op("act", lambda e: e.activation(out=out, in_=in_, func=func, **kw), reads, writes)

    def copy(self, eng, out, in_, reads, writes):
        if eng == "act":
            return self.s.op("act", lambda e: e.activation(out=out, in_=in_, func=AF.Copy), reads, writes)
        return self.s.op(eng, lambda e: e.tensor_copy(out=out, in_=in_), reads, writes)

    def mm(self, out, lhsT, rhs, start, stop, reads, writes):
        return self.s.op("pe", lambda e: e.matmul(out, lhsT=lhsT, rhs=rhs, start=start, stop=stop), reads, writes)

    def tr(self, out, in_, ident, reads, writes):
        return self.s.op("pe", lambda e: e.transpose(out, in_, ident), reads, writes)

    def memset(self, eng, ap, val, writes):
        return self.s.op(eng, lambda e: e.memset(ap, val), [], writes)

    def alt(self):
        self.rr ^= 1
        return "dve" if self.rr else "act"


def make_consts(k):
    s = k.s
    c = {}
    c["ident"] = s.sb([128, 128], BF16, "ident")
    c["identf"] = s.sb([128, 128], F32, "identf")
    c["swapf"] = s.sb([128, 128], F32, "swapf")
    c["iota"] = s.sb([128, 512], F32, "iota")
    c["ones"] = s.sb([128, 128], BF16, "ones")
    c["sgn"] = s.sb([128, 1], F32, "sgn")
    tst = contextlib.ExitStack()
    tmpf = s.sb([128, 128], F32, "tmpf", tst)
    iotai = s.sb([128, 512], I32, "iotai", tst)
    k.memset("pool", c["ident"][:], 1.0, [c["ident"]])
    s.op("pool", lambda e: e.affine_select(out=c["ident"][:], in_=c["ident"][:], pattern=[[-1, 128]],
                                           compare_op=ALU.is_equal, fill=0.0, base=0, channel_multiplier=1),
         [c["ident"]], [c["ident"]])
    k.copy("pool", c["identf"][:], c["ident"][:], [c["ident"]], [c["identf"]])
    k.memset("pool", c["swapf"][:], 1.0, [c["swapf"]])
    k.memset("pool", tmpf[:], 1.0, [tmpf])
    s.op("pool", lambda e: e.affine_select(out=c["swapf"][:], in_=c["swapf"][:], pattern=[[1, 128]],
                                           compare_op=ALU.is_equal, fill=0.0, base=-64, channel_multiplier=-1),
         [c["swapf"]], [c["swapf"]])
    s.op("pool", lambda e: e.affine_select(out=tmpf[:], in_=tmpf[:], pattern=[[1, 128]],
                                           compare_op=ALU.is_equal, fill=0.0, base=64, channel_multiplier=-1),
         [tmpf], [tmpf])
    k.tt("pool", c["swapf"][:], c["swapf"][:], tmpf[:], ALU.add, [c["swapf"], tmpf], [c["swapf"]])
    s.op("pool", lambda e: e.iota(iotai[:], pattern=[[1, 512]], base=0, channel_multiplier=0), [], [iotai])
    k.copy("pool", c["iota"][:], iotai[:], [iotai], [c["iota"]])
    k.memset("pool", c["ones"][:], 1.0, [c["ones"]])
    k.memset("pool", c["sgn"][:], 1.0, [c["sgn"]])
    s.op("pool", lambda e: e.affine_select(out=c["sgn"][:], in_=c["sgn"][:], pattern=[[0, 1]],
                                           compare_op=ALU.is_ge, fill=-1.0, base=63, channel_multiplier=-1),
         [c["sgn"]], [c["sgn"]])
    s.barrier()
    tst.close()
    return c


class WStream:
    def __init__(self, k, nk, ncols, st, nbuf=2, cast_eng="pool"):
        self.k = k
        self.nbuf = nbuf
        self.wb = [k.s.sb([128, nk, ncols], BF16, "wbf", st) for _ in range(nbuf)]
        self.i = 0

    def load(self, w_ap):
        i = self.i % self.nbuf
        self.i += 1
        n = w_ap.shape[1]
        self.k.s.wload(self.wb[i][:, :, 0:n], w_ap.rearrange("(k p) c -> p k c", p=128), [self.wb[i]], partial=False)
        return self.wb[i]


def linear_fm(k, ws, w_dram, col0, ncol, xT, nk, psums, evac, slab=256, ntok=S):
    slabs = [(c0, min(slab, ncol - c0)) for c0 in range(0, ncol, slab)]
    nxt = ws.load(w_dram[:, col0 + slabs[0][0]: col0 + slabs[0][0] + slabs[0][1]])
    pi = 0
    for si, (c0, n) in enumerate(slabs):
        cur = nxt
        if si + 1 < len(slabs):
            c1, n1 = slabs[si + 1]
            nxt = ws.load(w_dram[:, col0 + c1: col0 + c1 + n1])
        for mo in range(0, n, 128):
            mw = min(128, n - mo)
            m = (c0 + mo) // 128
            for tc in range(ntok // 512):
                ps = psums[pi % len(psums)]
                pi += 1
                for kk in range(nk):
                    k.mm(ps[0:mw, :], cur[:, kk, mo:mo + mw], xT[:, kk, tc * 512:(tc + 1) * 512],
                         kk == 0, kk == nk - 1, [cur, xT], [ps])
                evac(m, tc, ps, mw)


def phase_norm(k, c, x_dram, gcol, hT_dram, ntiles=NT):
    s = k.s
    with contextlib.ExitStack() as st:
        xb = [s.sb([128, D], F32, "xb", st) for _ in range(3)]
        junk = s.sb([128, D], BF16, "junk", st)
        xs = [s.sb([128, D], BF16, "xs", st) for _ in range(3)]
        ss = s.sb([128, ntiles], F32, "ss", st)
        rs = s.sb([128, ntiles], F32, "rs", st)
        ho = [s.sb([128, KD, 512], BF16, "ho", st) for _ in range(2)]
        pst = [s.ps([128, KD, 128], BF16, "pst", st) for _ in range(2)]
        ssb = [s.buf("ssb") for _ in range(ntiles)]
        rsb = [s.buf("rsb") for _ in range(ntiles)]
        k.memset("pool", ss[:], 0.0, ssb)
        def st1a(t):
            x_t = xb[t % 3]
            s.dma(x_t[:], x_dram[t * 128:(t + 1) * 128, :], writes=[x_t], key=f"xb{t % 3}")
            k.act(junk[:], x_t[:], AF.Square, [x_t], [junk, ssb[t]], accum_out=ss[:, t:t + 1])
            k.act(rs[:, t:t + 1], ss[:, t:t + 1], AF.Sqrt, [ssb[t]], [rsb[t]], bias=1e-6, scale=1.0 / D)
            s.op("dve", lambda e, t=t: e.reciprocal(out=rs[:, t:t + 1], in_=rs[:, t:t + 1]), [rsb[t]], [rsb[t]])
            xs_t = xs[t % 3]
            k.ts("dve", xs_t[:], x_t[:], rs[:, t:t + 1], None, ALU.mult, None, [x_t, rsb[t]], [xs_t])

        def st1b(t):
            xs_t = xs[t % 3]
            p = pst[t % 2]
            for kk in range(KD):
                k.tr(p[:, kk, :], xs_t[:, kk * 128:(kk + 1) * 128], c["ident"][:], [xs_t, c["ident"]], [p])

        def st2(t):
            p = pst[t % 2]
            h = ho[(t // 4) % 2]
            k.tt("dve", h[:, :, (t % 4) * 128:(t % 4 + 1) * 128], p[:],
                 gcol.unsqueeze(2).to_broadcast([128, KD, 128]), ALU.mult, [p], [h])
            if t % 4 == 3:
                tc = t // 4
                s.dma(hT_dram[:, :, tc * 512:(tc + 1) * 512], h[:], reads=[h], key=f"ho{tc % 2}", eng="act")
        st1a(0)
        if ntiles > 1:
            st1a(1)
        st1b(0)
        for t in range(ntiles):
            if t + 2 < ntiles:
                st1a(t + 2)
            if t + 1 < ntiles:
                st1b(t + 1)
            st2(t)
        s.barrier()


def frac_sincos(k, st, t_turns, n, reads, name="sc"):
    s = k.s
    ti = s.sb([128, n], I32, name + "_i", st)
    f = s.sb([128, n], F32, name + "_f", st)
    sn = s.sb([128, n], F32, name + "_s", st)
    cs = s.sb([128, n], F32, name + "_c", st)
    k.copy("dve", ti[:], t_turns[:], reads, [ti])
    k.tt("dve", f[:], t_turns[:], ti[:], ALU.subtract, reads + [ti], [f])
    k.act(sn[:], f[:], AF.Sin, [f], [sn], scale=SIN_SCALE)
    k.ts("dve", f[:], t_turns[:], 0.25, None, ALU.add, None, reads, [f])
    k.copy("dve", ti[:], f[:], [f], [ti])
    k.tt("dve", f[:], f[:], ti[:], ALU.subtract, [f, ti], [f])
    k.act(cs[:], f[:], AF.Sin, [f], [cs], scale=SIN_SCALE)
    return sn, cs


def phase_s5(k, c, hT_dram, w_in_l, p5, w_glu_l, bglu_col, yaT_dram):
    s = k.s
    Stager._id = 0
    WStream._id = 0
    with contextlib.ExitStack() as st0:
        Bw = s.sb([128, 16, 2, 128], BF16, "Bw", st0)
        Cp1 = s.sb([128, 32, 64], BF16, "Cp1", st0)
        Cp2 = s.sb([128, 32, 64], BF16, "Cp2", st0)
        Dmp = s.sb([128, 16, 64], BF16, "Dmp", st0)
        rcol = s.sb([128, 32], F32, "rcol", st0)
        thcol = s.sb([128, 32], F32, "thcol", st0)
        Rg = s.sb([128, 32, 128], F32, "Rg", st0)
        yT = s.sb([128, 4, S], BF16, "yT", st0)
        with contextlib.ExitStack() as st:
            n = 1024

            def ld(name, ap, shape):
                t = s.sb(shape, F32, name, st)
                s.dma(t[:], ap, writes=[t], key="s5" + name)
                return t
            bre = ld("bre", p5["bre"].rearrange("p a b -> p (a b)"), [128, n])
            bim = ld("bim", p5["bim"].rearrange("p a b -> p (a b)"), [128, n])
            ar = ld("ar", p5["ar"].rearrange("p a b -> p (a b)"), [128, n])
            ai = ld("ai", p5["ai"].rearrange("p a b -> p (a b)"), [128, n])
            ls = ld("ls", p5["ls"].rearrange("p a b -> p (a b)"), [128, n])
            stt = ld("stt", p5["st"], [128, 3, 32])
            c1 = ld("c1", p5["c1"], [128, 32, 16])
            c2 = ld("c2", p5["c2"], [128, 32, 16])
            dm = ld("dm", p5["dm"], [128, 16, 64])
            t1 = s.sb([128, n], F32, "t1", st)
            t2 = s.sb([128, n], F32, "t2", st)
            t3 = s.sb([128, n], F32, "t3", st)
            t4 = s.sb([128, n], F32, "t4", st)
            tq = s.sb([128, n], F32, "tq", st)
            k.act(t1[:], ls[:], AF.Exp, [ls], [t1])
            k.tt("dve", tq[:], ai[:], t1[:], ALU.mult, [ai, t1], [tq])
            k.ts("dve", tq[:], tq[:], 1.0 / TWO_PI, None, ALU.mult, None, [tq], [tq])
            k.tt("dve", t1[:], ar[:], t1[:], ALU.mult, [ar, t1], [t1])
            k.act(t1[:], t1[:], AF.Exp, [t1], [t1])
            sn, cs = frac_sincos(k, st, tq, n, [tq], "sc1")
            lbi = t2
            nr = t3
            k.tt("dve", lbi[:], t1[:], sn[:], ALU.mult, [t1, sn], [lbi])
            k.tt("dve", nr[:], t1[:], cs[:], ALU.mult, [t1, cs], [nr])
            k.ts("dve", nr[:], nr[:], -1.0, None, ALU.add, None, [nr], [nr])
            k.tt("dve", t1[:], ar[:], ar[:], ALU.mult, [ar], [t1])
            k.tt("dve", t4[:], ai[:], ai[:], ALU.mult, [ai], [t4])
            k.tt("dve", t1[:], t1[:], t4[:], ALU.add, [t1, t4], [t1])
            s.op("dve", lambda e: e.reciprocal(out=t1[:], in_=t1[:]), [t1], [t1])
            cr = sn
            ci = cs
            k.tt("dve", cr[:], nr[:], ar[:], ALU.mult, [nr, ar], [cr])
            k.tt("dve", t4[:], lbi[:], ai[:], ALU.mult, [lbi, ai], [t4])
            k.tt("dve", cr[:], cr[:], t4[:], ALU.add, [cr, t4], [cr])
            k.tt("dve", cr[:], cr[:], t1[:], ALU.mult, [cr, t1], [cr])
            k.tt("dve", ci[:], lbi[:], ar[:], ALU.mult, [lbi, ar], [ci])
            k.tt("dve", t4[:], nr[:], ai[:], ALU.mult, [nr, ai], [t4])
            k.tt("dve", ci[:], ci[:], t4[:], ALU.subtract, [ci, t4], [ci])
            k.tt("dve", ci[:], ci[:], t1[:], ALU.mult, [ci, t1], [ci])
            k.tt("dve", t2[:], cr[:], bre[:], ALU.mult, [cr, bre], [t2])
            k.tt("dve", t4[:], ci[:], bim[:], ALU.mult, [ci, bim], [t4])
            k.tt("dve", t2[:], t2[:], t4[:], ALU.subtract, [t2, t4], [t2])
            k.tt("dve", t3[:], cr[:], bim[:], ALU.mult, [cr, bim], [t3])
            k.tt("dve", t4[:], ci[:], bre[:], ALU.mult, [ci, bre], [t4])
            k.tt("dve", t3[:], t3[:], t4[:], ALU.add, [t3, t4], [t3])
            bbr3 = t2[:].rearrange("p (a b) -> p a b", b=64)
            bbi3 = t3[:].rearrange("p (a b) -> p a b", b=64)
            k.copy("dve", Bw[:, :, 0, 0:64], bbr3, [t2], [Bw])
            k.copy("dve", Bw[:, :, 0, 64:128], bbi3, [t3], [Bw])
            k.copy("dve", Bw[:, :, 1, 0:64], bbi3, [t3], [Bw])
            k.ts("dve", Bw[:, :, 1, 64:128], bbr3, -1.0, None, ALU.mult, None, [t2], [Bw])
            k.memset("pool", Cp1[:], 0.0, [Cp1])
            k.memset("pool", Cp2[:], 0.0, [Cp2])
            cp1v = Cp1[:].rearrange("p (a q e) w -> p a q e w", a=4, q=4, e=2)
            cp2v = Cp2[:].rearrange("p (a q e) w -> p a q e w", a=4, q=4, e=2)
            c1v = c1[:].rearrange("p (a q e) w -> p a q e w", a=4, q=4, e=2)
            c2v = c2[:].rearrange("p (a q e) w -> p a q e w", a=4, q=4, e=2)
            for e_ in range(2):
                for (q0, q1, off) in ((0, 3, 16 * e_), (3, 4, 32 + 16 * e_)):
                    k.ts("dve", cp1v[:, :, q0:q1, e_, off:off + 16], c1v[:, :, q0:q1, e_, :], c["sgn"][:, 0:1], None,
                         ALU.mult, None, [c1, c["sgn"]], [Cp1])
                    k.ts("dve", cp2v[:, :, q0:q1, e_, off:off + 16], c2v[:, :, q0:q1, e_, :], -1.0, None,
                         ALU.mult, None, [c2], [Cp2])
            k.copy("dve", Dmp[:], dm[:], [dm], [Dmp])
            u1 = s.sb([128, 32], F32, "u1", st)
            u2 = s.sb([128, 32], F32, "u2", st)
            k.act(u1[:], stt[:, 2, :], AF.Exp, [stt], [u1])
            k.tt("dve", u2[:], stt[:, 0, :], u1[:], ALU.mult, [stt, u1], [u2])
            k.act(rcol[:], u2[:], AF.Exp, [u2], [rcol])
            k.tt("dve", thcol[:], stt[:, 1, :], u1[:], ALU.mult, [stt, u1], [thcol])
            k.ts("dve", thcol[:], thcol[:], 1.0 / TWO_PI, None, ALU.mult, None, [thcol], [thcol])
            k.ts("dve", u2[:], thcol[:], 512.0, None, ALU.mult, None, [thcol], [u2])
            s512, c512 = frac_sincos(k, st, u2, 32, [u2], "sc2")
            k.ts("dve", s512[:], s512[:], c["sgn"][:, 0:1], None, ALU.mult, None, [s512, c["sgn"]], [s512])
            for g in range(32):
                k.ts("dve", Rg[:, g, :], c["identf"][:], c512[:, g:g + 1], None, ALU.mult, None,
                     [c["identf"], c512], [Rg])
                k.stt("dve", Rg[:, g, :], c["swapf"][:], s512[:, g:g + 1], Rg[:, g, :], ALU.mult, ALU.add,
                      [c["swapf"], s512, Rg], [Rg])
            s.barrier()
        with contextlib.ExitStack() as st:
            uT = s.sb([128, 4, S], BF16, "uT", st)
            with contextlib.ExitStack() as st2:
                ws = WStream(k, KD, 256, st2)
                wu = [ws.load(w_in_l[:, OFF_U + i * 256: OFF_U + (i + 1) * 256]) for i in range(2)]
                hc = [s.sb([128, KD, 512], BF16, "hc", st2) for _ in range(2)]
                psu = [s.ps([128, 512], F32, "psu", st2) for _ in range(4)]
                pi = 0
                s.dma(hc[0][:], hT_dram[:, :, 0:512], writes=[hc[0]], key="hc0")
                for tc in range(NCH):
                    h = hc[tc % 2]
                    if tc + 1 < NCH:
                        s.dma(hc[(tc + 1) % 2][:], hT_dram[:, :, (tc + 1) * 512:(tc + 2) * 512], writes=[hc[(tc + 1) % 2]], key=f"hc{(tc + 1) % 2}")
                    for m in range(4):
                        ps = psu[pi % 4]
                        pi += 1
                        w = wu[m // 2]
                        mo = (m % 2) * 128
                        for kk in range(KD):
                            k.mm(ps[:], w[:, kk, mo:mo + 128], h[:, kk, :], kk == 0, kk == KD - 1, [w, h], [ps])
                        k.copy(k.alt(), uT[:, m, tc * 512:(tc + 1) * 512], ps[:], [ps], [uT])
                s.barrier()
            NSLOT = 2
            tabs = [[s.sb([128, 512], F32, "tab", st) for _ in range(2)] for _ in range(2 * NSLOT)]
            ttmp = s.sb([128, 512], F32, "ttmp", st)
            tti = s.sb([128, 512], I32, "tti", st)
            tf = s.sb([128, 512], F32, "tf", st)
            T1 = [[s.sb([128, 512], F32, "T1", st) for _ in range(2)] for _ in range(NSLOT)]
            T2 = [[s.sb([128, 512], F32, "T2", st) for _ in range(2)] for _ in range(NSLOT)]
            X = [[s.sb([128, 512], F32, "X", st) for _ in range(2)] for _ in range(NSLOT)]
            G = [[s.sb([128, 512], F32, "G", st) for _ in range(2)] for _ in range(NSLOT)]
            G1 = [[s.sb([128, 512], BF16, "G1", st) for _ in range(2)] for _ in range(NSLOT)]
            G2 = [[s.sb([128, 512], BF16, "G2", st) for _ in range(2)] for _ in range(NSLOT)]
            psP = [[s.ps([128, 512], F32, "psP", st) for _ in range(2)] for _ in range(NSLOT)]
            psY = [s.ps([128, 512], F32, "psY", st) for _ in range(2)]
            psH = [s.ps([128, 512], F32, "psH", st) for _ in range(NSLOT)]
            hcol = [[s.sb([128, 1], F32, "hcol", st) for _ in range(2)] for _ in range(NSLOT)]
            tb = {}

            def gen_tables(gp):
                for si in range(NSLOT):
                    g = gp * NSLOT + si
                    sn_t, cs_t = tabs[(gp % 2) * NSLOT + si]
                    k.ts("dve", ttmp[:], c["iota"][:], thcol[:, g:g + 1], None, ALU.mult, None,
                         [c["iota"], thcol], [ttmp])
                    k.copy("dve", tti[:], ttmp[:], [ttmp], [tti])
                    k.tt("dve", tf[:], ttmp[:], tti[:], ALU.subtract, [ttmp, tti], [tf])
                    k.act(sn_t[:], tf[:], AF.Sin, [tf], [sn_t], scale=SIN_SCALE)
                    k.ts("dve", tf[:], ttmp[:], 0.25, None, ALU.add, None, [ttmp], [tf])
                    k.copy("dve", tti[:], tf[:], [tf], [tti])
                    k.tt("dve", tf[:], tf[:], tti[:], ALU.subtract, [tf, tti], [tf])
                    k.act(cs_t[:], tf[:], AF.Sin, [tf], [cs_t], scale=SIN_SCALE)
                    tb[g] = (sn_t, cs_t)

            def info(gp):
                out = []
                for si in range(NSLOT):
                    g = gp * NSLOT + si
                    kc, hb, j = g // 8, (g % 8) // 4, g % 4
                    out.append((si, g, kc, kc * 4 + j, 64 * hb, tb[g][0], tb[g][1]))
                return out

            def stage_ab(gp, ci_):
                b = ci_ % 2
                tk = slice(ci_ * 512, (ci_ + 1) * 512)
                gi_ = info(gp)
                for (si, g, kc, idx, r0, sn_t, cs_t) in gi_:
                    p1, p2 = psP[si]
                    k.mm(p1[:], Bw[r0:r0 + 64, idx, 0, :], uT[r0:r0 + 64, kc, tk], True, True, [Bw, uT], [p1])
                    k.mm(p2[:], Bw[r0:r0 + 64, idx, 1, :], uT[r0:r0 + 64, kc, tk], True, True, [Bw, uT], [p2])
                for (si, g, kc, idx, r0, sn_t, cs_t) in gi_:
                    p1, p2 = psP[si]
                    k.tt("dve", T1[si][b][:], p1[:], cs_t[:], ALU.mult, [p1, cs_t], [T1[si][b]])
                    k.tt("dve", T2[si][b][:], p2[:], sn_t[:], ALU.mult, [p2, sn_t], [T2[si][b]])
                for (si, g, kc, idx, r0, sn_t, cs_t) in gi_:
                    k.tt(S5E[0], X[si][b][:], T1[si][b][:], T2[si][b][:], ALU.add, [T1[si][b], T2[si][b]], [X[si][b]])
                for (si, g, kc, idx, r0, sn_t, cs_t) in gi_:
                    if ci_ == 0:
                        init = 0.0
                        rd = []
                    else:
                        hc_ = hcol[si][(ci_ - 1) % 2]
                        init = hc_[:, 0:1]
                        rd = [hc_]
                    s.op("dve", lambda e, o=G[si][b], x=X[si][b], g=g, init=init: e.tensor_tensor_scan(
                        out=o[:], data0=rcol[:, g:g + 1].to_broadcast([128, 512]), data1=x[:],
                        initial=init, op0=ALU.mult, op1=ALU.add), [X[si][b], rcol] + rd, [G[si][b]])
                for (si, g, kc, idx, r0, sn_t, cs_t) in gi_:
                    if ci_ < NCH - 1:
                        k.mm(psH[si][:, 0:1], Rg[:, g, :], G[si][b][:, 511:512], True, True,
                             [Rg, G[si][b]], [psH[si]])
                        k.copy("act", hcol[si][ci_ % 2][:], psH[si][:, 0:1], [psH[si]], [hcol[si][ci_ % 2]])

            def stage_c(gp, ci_):
                b = ci_ % 2
                tk = slice(ci_ * 512, (ci_ + 1) * 512)
                gi_ = info(gp)
                for n_, (si, g, kc, idx, r0, sn_t, cs_t) in enumerate(gi_):
                    k.tt(S5E[1], G1[si][b][:], G[si][b][:], cs_t[:], ALU.mult, [G[si][b], cs_t], [G1[si][b]])
                    e2 = S5E[2] if n_ == 0 else S5E[3]
                    k.tt(e2, G2[si][b][:], G[si][b][:], sn_t[:], ALU.mult, [G[si][b], sn_t], [G2[si][b]])
                q_ = gp % 4
                pbase, W = ((0, 32), (32, 32), (64, 64), (64, 64))[q_]
                py = psY[ci_ % 2]
                o = py[pbase:pbase + W, :]
                for n_, (si, g, kc, idx, r0, sn_t, cs_t) in enumerate(gi_):
                    k.mm(o, Cp1[:, g, 0:W], G1[si][b][:], n_ == 0, False, [G1[si][b], Cp1], [py])
                    k.mm(o, Cp2[:, g, 0:W], G2[si][b][:], False, False, [G2[si][b], Cp2], [py])
                (si, g, kc, idx, r0, sn_t, cs_t) = gi_[0]
                k.mm(o, Dmp[r0:r0 + 64, gp, 0:W], uT[r0:r0 + 64, kc, tk], False, True, [uT, Dmp], [py])
                if q_ < 3:
                    k.copy("act", yT[pbase:pbase + W, kc, tk], o, [py], [yT])
                else:
                    k.tt("dve", yT[64:128, kc, tk], o, yT[64:128, kc, tk], ALU.add, [py, yT], [yT])

            NP = 32 // NSLOT
            gen_tables(0)
            for gp in range(NP):
                for i_ in range(NCH + 1):
                    if i_ < NCH:
                        stage_ab(gp, i_)
                    if i_ >= 1:
                        stage_c(gp, i_ - 1)
                    if i_ == 3 and gp + 1 < NP:
                        gen_tables(gp + 1)
            s.barrier()
        with contextlib.ExitStack() as st:
            gT = s.sb([128, 4, S], BF16, "gT", st)
            gbufs = [[s.buf("gTb") for _ in range(NCH)] for _ in range(4)]
            for tc in range(NCH):
                for m in range(4):
                    k.act(gT[:, m, tc * 512:(tc + 1) * 512], yT[:, m, tc * 512:(tc + 1) * 512], AF.Gelu_apprx_tanh,
                          [yT], [gbufs[m][tc]])
            ws = WStream(k, 4, 512, st)
            wg = ws.load(w_glu_l)
            psz = [s.ps([128, 512], F32, "psz", st) for _ in range(2)]
            sg = [s.sb([128, 512], BF16, "sg", st) for _ in range(2)]
            yo = [s.sb([128, 512], BF16, "yo", st) for _ in range(2)]
            i = 0
            for tc in range(NCH):
                tk = slice(tc * 512, (tc + 1) * 512)
                for m in range(4):
                    ps = psz[i % 2]
                    for kk in range(4):
                        k.mm(ps[:], wg[:, kk, m * 128:(m + 1) * 128], gT[:, kk, tk], kk == 0, kk == 3, [wg, gbufs[kk][tc]], [ps])
                    k.act(sg[i % 2][:], ps[:], AF.Sigmoid, [ps], [sg[i % 2]], bias=bglu_col[:, m:m + 1], scale=1.0)
                    k.tt("dve", yo[i % 2][:], gT[:, m, tk], sg[i % 2][:], ALU.mult, [gbufs[m][tc], sg[i % 2]], [yo[i % 2]])
                    s.dma(yaT_dram[:, m, tk], yo[i % 2][:], reads=[yo[i % 2]], key=f"yo{i % 2}", eng="act")
                    i += 1
            s.barrier()


class Stager:
    def __init__(self, k, nk, ncols, st, nbuf=2, cast_eng="pool"):
        self.k = k
        self.i = 0

    def load(self, w_ap, dst_tile, dst_ap, rows=128):
        self.k.s.wload(dst_ap, w_ap.rearrange("(k p) c -> p k c", p=rows), [dst_tile])


def phase_gla(k, c, hT_dram, w_in_l, w_gate_l, b_gate_l, gout_col, ybT_dram, stop_after=9, sub=99):
    s = k.s
    Stager._id = 0
    WStream._id = 0
    with contextlib.ExitStack() as st0:
        qT = s.sb([128, 2, S], BF16, "qT", st0)
        kT = s.sb([128, 2, S], BF16, "kT", st0)
        k_tm = s.sb([128, NT, 256], BF16, "k_tm", st0)
        v_tm = s.sb([128, NT, 512], BF16, "v_tm", st0)
        lrT = s.sb([16, S], BF16, "lrT", st0)
        o_n = s.sb([128, NT, 512], BF16, "o_n", st0)
        triU = s.sb([128, 128], BF16, "triU", st0)
        triS = s.sb([128, 128], BF16, "triS", st0)
        m01 = s.sb([128, 128], BF16, "m01", st0)
        k.memset("pool", triU[:], -1.0 / 16.0, [triU])
        s.op("pool", lambda e: e.affine_select(out=triU[:], in_=triU[:], pattern=[[1, 128]], compare_op=ALU.is_ge,
                                               fill=0.0, base=0, channel_multiplier=-1), [triU], [triU])
        k.memset("pool", triS[:], -1.0 / 16.0, [triS])
        s.op("pool", lambda e: e.affine_select(out=triS[:], in_=triS[:], pattern=[[-1, 128]], compare_op=ALU.is_ge,
                                               fill=0.0, base=-1, channel_multiplier=1), [triS], [triS])
        k.memset("pool", m01[:], 1.0, [m01])
        s.op("pool", lambda e: e.affine_select(out=m01[:], in_=m01[:], pattern=[[1, 128]], compare_op=ALU.is_ge,
                                               fill=0.0, base=0, channel_multiplier=-1), [m01], [m01])
        wgate = s.sb([16, 256], BF16, "wgate", st0)
        bgate = s.sb([1, 256], BF16, "bgate", st0)
        with contextlib.ExitStack() as st:
            sg = Stager(k, KD, 256, st)
            wq = s.sb([128, KD, 256], BF16, "wq", st)
            wk = s.sb([128, KD, 256], BF16, "wk", st)
            wv = s.sb([128, KD, 512], BF16, "wv", st)
            wl = s.sb([128, KD, 16], BF16, "wl", st)
            sg.load(w_in_l[:, OFF_QG:OFF_QG + 256], wq, wq[:])
            sg.load(w_in_l[:, OFF_KG:OFF_KG + 256], wk, wk[:])
            sg.load(w_in_l[:, OFF_VG:OFF_VG + 256], wv, wv[:, :, 0:256])
            sg.load(w_in_l[:, OFF_VG + 256:OFF_VG + 512], wv, wv[:, :, 256:512])
            sg.load(w_in_l[:, OFF_LR:OFF_LR + 16], wl, wl[:])
            sg.load(w_gate_l, wgate, wgate[:].unsqueeze(1), rows=16)
            sg.load(b_gate_l, bgate, bgate[:].unsqueeze(1), rows=1)
            hc = [s.sb([128, KD, 512], BF16, "hc", st) for _ in range(2)]
            psf = [s.ps([128, 512], F32, "psf", st) for _ in range(3)]
            pst = [s.ps([128, 512], F32, "pst", st) for _ in range(3)]
            fi = 0
            ti = 0
            s.dma(hc[0][:], hT_dram[:, :, 0:512], writes=[hc[0]], key="ghc0")
            for tc in range(NCH):
                h = hc[tc % 2]
                tk = slice(tc * 512, (tc + 1) * 512)
                if tc + 1 < NCH:
                    s.dma(hc[(tc + 1) % 2][:], hT_dram[:, :, (tc + 1) * 512:(tc + 2) * 512], writes=[hc[(tc + 1) % 2]], key=f"ghc{(tc + 1) % 2}")
                for (w, dstT) in ((wq, qT), (wk, kT)):
                    for m in range(2):
                        ps = psf[fi % 3]
                        fi += 1
                        for kk in range(KD):
                            k.mm(ps[:], w[:, kk, m * 128:(m + 1) * 128], h[:, kk, :], kk == 0, kk == KD - 1, [w, h], [ps])
                        k.copy(k.alt(), dstT[:, m, tk], ps[:], [ps], [dstT])
                ps = psf[fi % 3]
                fi += 1
                for kk in range(KD):
                    k.mm(ps[0:16, :], wl[:, kk, :], h[:, kk, :], kk == 0, kk == KD - 1, [wl, h], [ps])
                k.copy(k.alt(), lrT[:, tk], ps[0:16, :], [ps], [lrT])
                for t in range(4):
                    T = tc * 4 + t
                    ps = pst[ti % 3]
                    ti += 1
                    for kk in range(KD):
                        k.mm(ps[:, 0:256], h[:, kk, t * 128:(t + 1) * 128], wk[:, kk, :], kk == 0, kk == KD - 1, [wk, h], [ps])
                    k.copy(k.alt(), k_tm[:, T, :], ps[:, 0:256], [ps], [k_tm])
                    ps = pst[ti % 3]
                    ti += 1
                    for kk in range(KD):
                        k.mm(ps[:], h[:, kk, t * 128:(t + 1) * 128], wv[:, kk, :], kk == 0, kk == KD - 1, [wv, h], [ps])
                    k.copy(k.alt(), v_tm[:, T, :], ps[:], [ps], [v_tm])
            s.barrier()
        if stop_after < 2:
            return
        with contextlib.ExitStack() as st:
            Sst = s.sb([128, 2, 128], F32, "Sst", st)
            Sbf = [s.sb([128, 2, 128], BF16, "Sbf", st) for _ in range(2)]
            e1 = [s.sb([128, 256], F32, "e1", st) for _ in range(2)]
            L = [s.sb([128, 256], F32, "L", st) for _ in range(2)]
            Lh = [s.sb([128, 256], BF16, "Lh", st) for _ in range(2)]
            Ll = [s.sb([128, 256], BF16, "Ll", st) for _ in range(2)]
            eb = [s.sb([128, 2, 128], F32, "eb", st) for _ in range(2)]
            enb = [s.sb([128, 2, 128], F32, "enb", st) for _ in range(2)]
            erc = [s.sb([128, 256], F32, "erc", st) for _ in range(2)]
            qe0 = [s.sb([128, 2, 128], BF16, "qe0", st) for _ in range(2)]
            qe1 = [s.sb([128, 2, 128], BF16, "qe1", st) for _ in range(2)]
            c0col = s.sb([128, 1], F32, "c0col", st)
            c1col = s.sb([128, 1], F32, "c1col", st)
            k.ts("dve", c0col[:], c["sgn"][:], 0.0625, 0.0625, ALU.mult, ALU.add, [c["sgn"]], [c0col])
            k.ts("dve", c1col[:], c["sgn"][:], -0.0625, 0.0625, ALU.mult, ALU.add, [c["sgn"]], [c1col])
            ke = [s.sb([128, 2, 128], BF16, "ke", st) for _ in range(2)]
            kend0 = [s.sb([128, 2, 128], BF16, "kend0", st) for _ in range(2)]
            kend1 = [s.sb([128, 2, 128], BF16, "kend1", st) for _ in range(2)]
            for q_ in range(2):
                k.memset("pool", kend0[q_][:], 0.0, [kend0[q_]])
                k.memset("pool", kend1[q_][:], 0.0, [kend1[q_]])
            Am = [s.sb([128, 4, 128], BF16, "Am", st) for _ in range(2)]
            junk = s.sb([128, 128], BF16, "gjunk", st)
            ssq = [s.sb([128, 4], F32, "ssq", st) for _ in range(2)]
            psZ = s.ps([128, 512], F32, "psZ", st)
            psR = s.ps([128, 512], F32, "psR", st)
            psB = s.ps([128, 4, 128], F32, "psB", st)
            psA = [s.ps([128, 4, 128], F32, "psA", st) for _ in range(2)]
            psO = [s.ps([128, 4, 128], F32, "psO", st) for _ in range(2)]
            psS = s.ps([128, 4, 128], F32, "psS", st)
            zb = s.buf("zb")
            rcb = s.buf("rcb")
            k.memset("pool", Sst[:], 0.0, [Sst])
            def f1(n):
                b = n % 2
                tk = slice(n * 128, (n + 1) * 128)
                k.mm(psZ[:, 0:256], lrT[0:16, tk], wgate[:], True, False, [lrT, wgate], [zb])
                k.mm(psZ[:, 0:256], c["ones"][0:1, :], bgate[:], False, True, [c["ones"], bgate], [zb])
                k.act(e1[b][:], psZ[:, 0:256], AF.Exp, [zb], [e1[b]], scale=-1.0)
                k.act(L[b][:], e1[b][:], AF.Ln, [e1[b]], [L[b]], bias=1.0)
                k.copy("dve", Lh[b][:], L[b][:], [L[b]], [Lh[b]])
                k.tt("dve", Ll[b][:], L[b][:], Lh[b][:], ALU.subtract, [L[b], Lh[b]], [Ll[b]])

            def f2(n):
                b = n % 2
                tk = slice(n * 128, (n + 1) * 128)
                for hp in range(2):
                    k.mm(psB[:, hp, :], Lh[b][:, hp * 128:(hp + 1) * 128], triU[:], True, False, [Lh[b], triU], [psB])
                    k.mm(psB[:, hp, :], Ll[b][:, hp * 128:(hp + 1) * 128], triU[:], False, True, [Ll[b], triU], [psB])
                k.mm(psR[:, 0:256], triS[:], Lh[b][:], True, False, [triS, Lh[b]], [rcb])
                k.mm(psR[:, 0:256], triS[:], Ll[b][:], False, True, [triS, Ll[b]], [rcb])
                k.act(eb[b][:], psB[:, 0:2, :], AF.Exp, [psB], [eb[b]])
                k.act(enb[b][:], psB[:, 0:2, :], AF.Exp, [psB], [enb[b]], scale=-1.0)
                k.act(erc[b][:], psR[:, 0:256], AF.Exp, [rcb], [erc[b]])
                k.stt("dve", qe0[b][:], qT[:, :, tk], c0col[:, 0:1], eb[b][:], ALU.mult, ALU.mult, [qT, eb[b], c0col], [qe0[b]])
                k.stt("dve", qe1[b][:], qT[:, :, tk], c1col[:, 0:1], eb[b][:], ALU.mult, ALU.mult, [qT, eb[b], c1col], [qe1[b]])
                k.tt("dve", ke[b][:], kT[:, :, tk], enb[b][:], ALU.mult, [kT, enb[b]], [ke[b]])
                kv = k_tm[:, n, :].rearrange("p (a b d) -> p a b d", a=2, b=2)
                ev = erc[b][:].rearrange("p (a b d) -> p a b d", a=2, b=2)
                k.tt("dve", kend0[b][:, :, 0:64], kv[:, :, 0, :], ev[:, :, 0, :], ALU.mult, [k_tm, erc[b]], [kend0[b]])
                k.tt("dve", kend1[b][:, :, 64:128], kv[:, :, 1, :], ev[:, :, 1, :], ALU.mult, [k_tm, erc[b]], [kend1[b]])

            def f3(n):
                b = n % 2
                qes = (qe0[b], qe1[b])
                pa = psA[b]
                for h in range(4):
                    hp = h // 2
                    k.mm(pa[:, h, :], ke[b][:, hp, :], qes[h % 2][:, hp, :], True, True, [ke[b], qes[h % 2]], [pa])
                k.tt("dve", Am[b][:], pa[:], m01[:].unsqueeze(1).to_broadcast([128, 4, 128]), ALU.mult, [pa, m01], [Am[b]])

            def b1(n):
                b = n % 2
                qes = (qe0[b], qe1[b])
                po = psO[b]
                sprev = Sbf[(n - 1) % 2]
                for h in range(4):
                    hp = h // 2
                    k.mm(po[:, h, :], Am[b][:, h, :], v_tm[:, n, h * 128:(h + 1) * 128], True, n == 0, [Am[b], v_tm], [po])
                    if n > 0:
                        k.mm(po[:, h, :], qes[h % 2][:, hp, :], sprev[:, hp, :], False, True,
                             [qes[h % 2], sprev], [po])

            def b2(n):
                b = n % 2
                if n < NT - 1:
                    for hp in range(2):
                        k.mm(psS[:, hp, :], kend0[b][:, hp, :], v_tm[:, n, (2 * hp) * 128:(2 * hp + 1) * 128],
                             True, False, [kend0[b], v_tm], [psS])
                        k.mm(psS[:, hp, :], kend1[b][:, hp, :], v_tm[:, n, (2 * hp + 1) * 128:(2 * hp + 2) * 128],
                             False, True, [kend1[b], v_tm], [psS])
                    for hp in range(2):
                        k.stt("dve", Sst[:, hp, :], Sst[:, hp, :], eb[b][:, hp, 127:128], psS[:, hp, :], ALU.mult, ALU.add,
                              [Sst, eb[b], psS], [Sst])
                    k.copy("pool", Sbf[n % 2][:], Sst[:], [Sst], [Sbf[n % 2]])

            def b3(n):
                b = n % 2
                po = psO[b]
                k.memset("pool", ssq[b][:], 0.0, [ssq[b]])
                for h in range(4):
                    k.act(junk[:], po[:, h, :], AF.Square, [po], [junk, ssq[b]], accum_out=ssq[b][:, h:h + 1])
                k.act(ssq[b][:], ssq[b][:], AF.Ln, [ssq[b]], [ssq[b]], bias=1e-6, scale=1.0 / 128.0)
                k.act(ssq[b][:], ssq[b][:], AF.Exp, [ssq[b]], [ssq[b]], scale=-0.5)
                k.tt("dve", o_n[:, n, :].rearrange("p (h e) -> p h e", e=128), po[:],
                     ssq[b][:].unsqueeze(2).to_broadcast([128, 4, 128]), ALU.mult, [po, ssq[b]], [o_n])

            f1(0)
            f2(0)
            f3(0)
            for n in range(NT):
                nx = n + 1 < NT
                if nx:
                    f1(n + 1)
                b1(n)
                if nx:
                    f2(n + 1)
                b2(n)
                if nx:
                    f3(n + 1)
                b3(n)
            s.barrier()
        if stop_after < 3:
            return
        with contextlib.ExitStack() as st:
            sg = Stager(k, KD, 256, st)
            wr = s.sb([128, KD, 512], BF16, "wr", st)
            sg.load(w_in_l[:, OFF_RG:OFF_RG + 256], wr, wr[:, :, 0:256])
            sg.load(w_in_l[:, OFF_RG + 256:OFF_RG + 512], wr, wr[:, :, 256:512])
            hc = [s.sb([128, KD, 512], BF16, "hc3", st) for _ in range(2)]
            psr = [s.ps([128, 512], F32, "psr", st) for _ in range(2)]
            psT = [s.ps([128, 8, 128], BF16, "psT", st) for _ in range(2)]
            sr = [s.sb([128, 512], BF16, "sr", st) for _ in range(2)]
            yo = [s.sb([128, 512], BF16, "yob", st) for _ in range(2)]
            i = 0
            s.dma(hc[0][:], hT_dram[:, :, 0:512], writes=[hc[0]], key="g3hc0")
            for tc in range(NCH):
                h = hc[tc % 2]
                tk = slice(tc * 512, (tc + 1) * 512)
                if tc + 1 < NCH:
                    s.dma(hc[(tc + 1) % 2][:], hT_dram[:, :, (tc + 1) * 512:(tc + 2) * 512], writes=[hc[(tc + 1) % 2]], key=f"g3hc{(tc + 1) % 2}")
                for m in range(4):
                    ps = psr[i % 2]
                    for kk in range(KD):
                        k.mm(ps[:], wr[:, kk, m * 128:(m + 1) * 128], h[:, kk, :], kk == 0, kk == KD - 1, [wr, h], [ps])
                    k.act(sr[i % 2][:], ps[:], AF.Silu, [ps], [sr[i % 2]])
                    pt = psT[i % 2]
                    for t in range(4):
                        k.tr(pt[:, t, :], o_n[:, tc * 4 + t, m * 128:(m + 1) * 128], c["ident"][:], [o_n, c["ident"]], [pt])
                    k.stt("dve", yo[i % 2][:], pt[:, 0:4, :].rearrange("p a b -> p (a b)"), gout_col, sr[i % 2][:], ALU.mult, ALU.mult,
                          [pt, sr[i % 2]], [yo[i % 2]])
                    s.dma(ybT_dram[:, m, tk], yo[i % 2][:], reads=[yo[i % 2]], key=f"ybo{i % 2}", eng="act")
                    i += 1
            s.barrier()


ATT_GROUPS = ((128, 1), (512, 4), (2048, 16))


def att_batches(dil):
    out = []
    if dil == 1:
        for b in range(8):
            out.append(([(0, 4 * b + q) for q in range(4)], ("flat", 0, 512 * b)))
    elif dil == 4:
        for r in range(4):
            for half in range(2):
                out.append(([(r, 4 * half + q) for q in range(4)], ("flat", r, 512 * half)))
    else:
        for b in range(8):
            out.append(([(2 * b, 0), (2 * b, 1), (2 * b + 1, 0), (2 * b + 1, 1)], ("pair", 2 * b, 0)))
    return out


def phase_attn(k, c, hT_dram, w_in_l, ycT_dram, hh_list=(0, 1, 2, 3)):
    s = k.s
    Stager._id = 0
    WStream._id = 0
    with contextlib.ExitStack() as st0:
        hT = s.sb([128, KD, S], BF16, "hTr", st0)
        hbufs = [s.buf("hTb") for _ in range(NCH)]
        for tc in range(NCH):
            s.dma(hT[:, :, tc * 512:(tc + 1) * 512], hT_dram[:, :, tc * 512:(tc + 1) * 512], writes=[hbufs[tc]], key=f"ahT{tc}")
        qT = [s.sb([128, S], BF16, "aqT", st0) for _ in range(2)]
        kT = [s.sb([128, S], BF16, "akT", st0) for _ in range(2)]
        vpm = [s.sb([128, 32, 128], BF16, "vpm", st0) for _ in range(2)]
        wqkv = [s.sb([128, KD, 384], BF16, "wqkv", st0) for _ in range(2)]
        sg = Stager(k, KD, 128, st0, nbuf=3, cast_eng=("pool", "pool", "dve"))
        nums = [s.sb([128, S], F32, "num", st0) for _ in range(2)]
        dens = [s.sb([128, S], F32, "den", st0) for _ in range(2)]
        Wc = [s.sb([128, 128], BF16, "Wc", st0) for _ in range(2)]
        Wp = [s.sb([128, 128], BF16, "Wp", st0) for _ in range(2)]
        Wt = s.sb([128, 128], F32, "Wt", st0)
        di = s.sb([128, 128], I32, "di", st0)
        dcur = s.sb([128, 128], F32, "dcur", st0)
        dprev = s.sb([128, 128], F32, "dprev", st0)
        s.op("pool", lambda e: e.iota(di[:], pattern=[[1, 128]], base=0, channel_multiplier=-1), [], [di])
        k.copy("pool", dcur[:], di[:], [di], [dcur])
        k.ts("dve", dprev[:], dcur[:], 128.0, None, ALU.add, None, [dcur], [dprev])
        k.ts("dve", dcur[:], dcur[:], 0.0, None, ALU.max, None, [dcur], [dcur])
        Ec = [s.sb([128, 4, 128], BF16, "Ec", st0)] * 2
        Ep = [s.sb([128, 4, 128], BF16, "Ep", st0)] * 2
        Pc = [s.sb([128, 4, 128], BF16, "Pc", st0) for _ in range(2)]
        Pp = [s.sb([128, 4, 128], BF16, "Pp", st0) for _ in range(2)]
        yo = [s.sb([128, 512], BF16, "ayo", st0) for _ in range(2)]
        pj = [s.ps([128, 512], F32, "pj", st0) for _ in range(2)]
        psSc = [s.ps([128, 4, 128], F32, "psSc", st0) for _ in range(2)]
        psSp = [s.ps([128, 4, 128], F32, "psSp", st0) for _ in range(2)]
        psN = s.ps([128, 4, 128], F32, "psN", st0)
        psD = s.ps([128, 4, 128], F32, "psD", st0)
        sc = 1.0 / math.sqrt(128.0)
        hcount = 0
        pji = 0
        bcount = 0
        heads = [(hh, g) for hh in hh_list for g in range(3)]
        pending = []

        def ldh(idx):
            hh_, g_ = heads[idx]
            hd_ = g_ * 4 + hh_
            w_ = wqkv[idx % 2]
            sg.load(w_in_l[:, OFF_QA + hd_ * 128: OFF_QA + (hd_ + 1) * 128], w_, w_[:, :, 0:128])
            sg.load(w_in_l[:, OFF_KA + hd_ * 128: OFF_KA + (hd_ + 1) * 128], w_, w_[:, :, 128:256])
            sg.load(w_in_l[:, OFF_VA + hd_ * 128: OFF_VA + (hd_ + 1) * 128], w_, w_[:, :, 256:384])
        ldh(0)
        for hi_, hh in enumerate(hh_list):
            num, den = nums[hi_ % 2], dens[hi_ % 2]
            for g, (window, dil) in enumerate(ATT_GROUPS):
                hd = g * 4 + hh
                i2 = hcount % 2
                hcount += 1
                w = wqkv[i2]
                if hcount < len(heads):
                    ldh(hcount)
                slope = 2.0 ** (-8.0 * (hd + 1) / 12.0)
                cf = slope * dil
                k.act(Wt[:], dcur[:], AF.Exp, [dcur], [Wt], scale=-cf)
                s.op("pool", lambda e, o=Wc[i2]: e.affine_select(out=o[:], in_=Wt[:], pattern=[[1, 128]], compare_op=ALU.is_ge,
                                                               fill=0.0, base=0, channel_multiplier=-1), [Wt], [Wc[i2]])
                k.act(Wt[:], dprev[:], AF.Exp, [dprev], [Wt], scale=-cf)
                s.op("pool", lambda e, o=Wp[i2]: e.affine_select(out=o[:], in_=Wt[:], pattern=[[-1, 128]], compare_op=ALU.is_ge,
                                                               fill=0.0, base=0, channel_multiplier=1), [Wt], [Wp[i2]])
                q_, k_, v_ = qT[i2], kT[i2], vpm[i2]
                for tc in range(NCH):
                    tk = slice(tc * 512, (tc + 1) * 512)
                    for (off, dst) in ((0, q_), (128, k_)):
                        ps = pj[pji % 2]
                        pji += 1
                        for kk in range(KD):
                            k.mm(ps[:], w[:, kk, off:off + 128], hT[:, kk, tk], kk == 0, kk == KD - 1, [w, hbufs[tc]], [ps])
                        k.copy(k.alt(), dst[:, tk], ps[:], [ps], [dst])
                nper = 32 // dil
                batches = att_batches(dil)

                def tsl(r, n):
                    st_ = r + 128 * dil * n
                    return slice(st_, st_ + 127 * dil + 1, dil)
                for blocks, _ in batches:
                    ps = pj[pji % 2]
                    pji += 1
                    for qi, (r, n) in enumerate(blocks):
                        for kk in range(KD):
                            k.mm(ps[:, qi * 128:(qi + 1) * 128], hT[:, kk, tsl(r, n)], w[:, kk, 256:384],
                                 kk == 0, kk == KD - 1, [w] + hbufs, [ps])
                    b0 = blocks[0][0] * nper + blocks[0][1]
                    if dil == 16:
                        b0 = blocks[0][0] * nper
                    k.copy(k.alt(), v_[:, b0:b0 + 4, :], ps[:].rearrange("p (a b) -> p a b", b=128), [ps], [v_])
                for blocks, (kind, r0, j0) in batches:
                    if pending:
                        pending.pop(0)()
                    bb = bcount % 2
                    bcount += 1
                    pc, pp = psSc[bb], psSp[bb]
                    has_prev = [n > 0 for (_, n) in blocks]
                    for qi, (r, n) in enumerate(blocks):
                        k.mm(pc[:, qi, :], k_[:, tsl(r, n)], q_[:, tsl(r, n)], True, True, [k_, q_], [pc])
                    for qi, (r, n) in enumerate(blocks):
                        if n > 0:
                            k.mm(pp[:, qi, :], k_[:, tsl(r, n - 1)], q_[:, tsl(r, n)], True, True, [k_, q_], [pp])
                    k.act(Ec[bb][:], pc[:], AF.Exp, [pc], [Ec[bb]], scale=sc)
                    k.tt("dve", Pc[bb][:], Ec[bb][:], Wc[i2][:].unsqueeze(1).to_broadcast([128, 4, 128]), ALU.mult,
                         [Ec[bb], Wc[i2]], [Pc[bb]])
                    pq = [qi for qi in range(4) if has_prev[qi]]
                    if pq:
                        q0, q1 = pq[0], pq[-1] + 1
                        if pq != list(range(q0, q1)):
                            for qi in pq:
                                k.act(Ep[bb][:, qi, :], pp[:, qi, :], AF.Exp, [pp], [Ep[bb]], scale=sc)
                                k.tt("dve", Pp[bb][:, qi, :], Ep[bb][:, qi, :], Wp[i2][:], ALU.mult, [Ep[bb], Wp[i2]], [Pp[bb]])
                        else:
                            k.act(Ep[bb][:, q0:q1, :], pp[:, q0:q1, :], AF.Exp, [pp], [Ep[bb]], scale=sc)
                            k.tt("dve", Pp[bb][:, q0:q1, :], Ep[bb][:, q0:q1, :],
                                 Wp[i2][:].unsqueeze(1).to_broadcast([128, q1 - q0, 128]), ALU.mult, [Ep[bb], Wp[i2]], [Pp[bb]])
                    for qi, (r, n) in enumerate(blocks):
                        bi = r * nper + n
                        k.mm(psN[:, qi, :], v_[:, bi, :], Pc[bb][:, qi, :], True, not has_prev[qi], [v_, Pc[bb]], [psN])
                        if has_prev[qi]:
                            k.mm(psN[:, qi, :], v_[:, bi - 1, :], Pp[bb][:, qi, :], False, True, [v_, Pp[bb]], [psN])
                    for qi, (r, n) in enumerate(blocks):
                        k.mm(psD[:, qi, :], c["ones"][:], Pc[bb][:, qi, :], True, not has_prev[qi], [c["ones"], Pc[bb]], [psD])
                        if has_prev[qi]:
                            k.mm(psD[:, qi, :], c["ones"][:], Pp[bb][:, qi, :], False, True, [c["ones"], Pp[bb]], [psD])
                    for acc, psx in ((num, psN), (den, psD)):
                        a3 = acc[:].rearrange("p (j r) -> p r j", r=dil)
                        if kind == "flat":
                            av = a3[:, r0, j0:j0 + 512]
                            pv = psx[:].rearrange("p a b -> p (a b)")
                        else:
                            av = a3[:, r0:r0 + 2, 0:256]
                            pv = psx[:].rearrange("p (a b) i -> p a (b i)", a=2)
                        if g == 0:
                            k.copy("dve", av, pv, [psx], [acc])
                        else:
                            k.tt("dve", av, pv, av, ALU.add, [psx, acc], [acc])
            def mk_fin(tc, hh=hh, num=num, den=den):
                def f():
                    tk = slice(tc * 512, (tc + 1) * 512)
                    k.act(den[:, tk], den[:, tk], AF.Ln, [den], [den])
                    k.act(den[:, tk], den[:, tk], AF.Exp, [den], [den], scale=-1.0)
                    y = yo[tc % 2]
                    k.tt("pool", y[:], num[:, tk], den[:, tk], ALU.mult, [num, den], [y])
                    s.dma(ycT_dram[:, hh, tk], y[:], reads=[y], key=f"ayo{tc % 2}", eng="sp")
                return f
            for tc in range(NCH):
                pending.append(mk_fin(tc))
        while pending:
            pending.pop(0)()
        s.barrier()


def phase_merge(k, c, hT_dram, yT_drams, w_in_l, w_branch_l, w_out_l, x_src, x_dst):
    s = k.s
    Stager._id = 0
    WStream._id = 0
    HALF = S // 2
    with contextlib.ExitStack() as st0:
        wo = s.sb([128, KD, D], BF16, "wo", st0)
        sgo = Stager(k, KD, 256, st0)
        hT = s.sb([128, KD, HALF], BF16, "mhT", st0)
        yT = [s.sb([128, 4, HALF], BF16, "myT", st0) for _ in range(3)]
        mT = s.sb([128, KD, HALF], BF16, "mT", st0)
        wg = [[s.sb([128, KD, 128], BF16, "wg", st0) for _ in range(3)] for _ in range(2)]
        wb = [[s.sb([128, 4, 128], BF16, "wb", st0) for _ in range(3)] for _ in range(2)]
        sg = Stager(k, KD, 128, st0, nbuf=3, cast_eng=("dve", "act"))
        sig = [s.sb([128, 512], BF16, "sig", st0) for _ in range(3)]
        acc = [s.sb([128, 512], F32, "macc", st0) for _ in range(2)]
        tmp = [s.sb([128, 512], F32, "mtmp", st0) for _ in range(2)]
        xt = [s.sb([128, D], F32, "mxt", st0) for _ in range(2)]
        psG = [s.ps([128, 512], F32, "psG", st0) for _ in range(3)]
        psP = [s.ps([128, 512], F32, "psP", st0) for _ in range(3)]
        psX = [s.ps([128, 512], F32, "psX", st0) for _ in range(2)]
        gi = 0
        hb = [s.buf("mhb") for _ in range(4)]
        yb = [[s.buf("myb") for _ in range(4)] for _ in range(3)]
        mbufs = [s.buf("mTb") for _ in range(KD)]
        for half in range(2):
            t0 = half * HALF
            for q in range(4):
                sl = slice(q * 512, (q + 1) * 512)
                gl = slice(t0 + q * 512, t0 + (q + 1) * 512)
                s.dma(hT[:, :, sl], hT_dram[:, :, gl], writes=[hb[q]], key=f"mh{q}")
                for n in range(3):
                    s.dma(yT[n][:, :, sl], yT_drams[n][:, :, gl], writes=[yb[n][q]], key=f"my{n}{q}")
            def ldw(m_, ws_):
                for n in range(3):
                    c0 = OFF_GT + n * D + m_ * 128
                    sg.load(w_in_l[:, c0:c0 + 128], wg[ws_][n], wg[ws_][n][:])
                    sg.load(w_branch_l[n, :, m_ * 128:(m_ + 1) * 128], wb[ws_][n], wb[ws_][n][:])
            if half == 0:
                ldw(0, gi % 2)
                for j in range(4):
                    sgo.load(w_out_l[:, j * 256:(j + 1) * 256], wo, wo[:, :, j * 256:(j + 1) * 256])
            for m in range(KD):
                ws = gi % 2
                gi += 1
                if m + 1 < KD:
                    ldw(m + 1, gi % 2)
                elif half == 0:
                    ldw(0, gi % 2)
                for q in range(4):
                    sl = slice(q * 512, (q + 1) * 512)
                    a = acc[q % 2]
                    for n in range(3):
                        pg, pp = psG[n], psP[n]
                        for kk in range(KD):
                            k.mm(pg[:], wg[ws][n][:, kk, :], hT[:, kk, sl], kk == 0, kk == KD - 1, [wg[ws][n], hb[q]], [pg])
                        for kk in range(4):
                            k.mm(pp[:], wb[ws][n][:, kk, :], yT[n][:, kk, sl], kk == 0, kk == 3, [wb[ws][n], yb[n][q]], [pp])
                        k.act(sig[n][:], pg[:], AF.Sigmoid, [pg], [sig[n]])
                        if n == 0:
                            k.tt("dve", a[:], pp[:], sig[n][:], ALU.mult, [pp, sig[n]], [a])
                        elif n == 1:
                            tq = tmp[0]
                            k.tt("dve", tq[:], pp[:], sig[n][:], ALU.mult, [pp, sig[n]], [tq])
                            k.tt("dve", a[:], a[:], tq[:], ALU.add, [a, tq], [a])
                        else:
                            tq = tmp[1]
                            k.tt("dve", tq[:], pp[:], sig[n][:], ALU.mult, [pp, sig[n]], [tq])
                            k.tt("pool", mT[:, m, sl], a[:], tq[:], ALU.add, [a, tq], [mbufs[m]])
            for t in range(HALF // 128):
                T = t0 // 128 + t
                x_t = xt[t % 2]
                s.dma(x_t[:], x_src[T * 128:(T + 1) * 128, :], writes=[x_t], key=f"mx{t % 2}")
                for hf in range(2):
                    ps = psX[hf]
                    for kk in range(KD):
                        k.mm(ps[:], mT[:, kk, t * 128:(t + 1) * 128], wo[:, kk, hf * 512:(hf + 1) * 512],
                             kk == 0, kk == KD - 1, [mbufs[kk], wo], [ps])
                    k.tt("dve", x_t[:, hf * 512:(hf + 1) * 512], ps[:], x_t[:, hf * 512:(hf + 1) * 512], ALU.add, [ps, x_t], [x_t])
                s.dma(x_dst[T * 128:(T + 1) * 128, :], x_t[:], reads=[x_t], key=f"mxo{t % 2}", eng="act")
        s.barrier()


def phase_memnorm(k, c, mem_dram, gcol, memT):
    s = k.s
    with contextlib.ExitStack() as st:
        xb = [s.sb([128, D], F32, "mxb", st) for _ in range(2)]
        junk = s.sb([128, D], BF16, "mjunk", st)
        xs = [s.sb([128, D], BF16, "mxs", st) for _ in range(2)]
        ss = s.sb([128, 2], F32, "mss", st)
        pst = [s.ps([128, KD, 128], BF16, "mpst", st) for _ in range(2)]
        k.memset("pool", ss[:], 0.0, [ss])
        for t in range(2):
            s.dma(xb[t][:], mem_dram[t * 128:(t + 1) * 128, :], writes=[xb[t]], key=f"mm{t}")
            k.act(junk[:], xb[t][:], AF.Square, [xb[t]], [junk, ss], accum_out=ss[:, t:t + 1])
        k.act(ss[:], ss[:], AF.Sqrt, [ss], [ss], bias=1e-6, scale=1.0 / D)
        s.op("dve", lambda e: e.reciprocal(out=ss[:], in_=ss[:]), [ss], [ss])
        for t in range(2):
            k.ts("dve", xs[t][:], xb[t][:], ss[:, t:t + 1], None, ALU.mult, None, [xb[t], ss], [xs[t]])
            for kk in range(KD):
                k.tr(pst[t][:, kk, :], xs[t][:, kk * 128:(kk + 1) * 128], c["ident"][:], [xs[t], c["ident"]], [pst[t]])
            k.tt("dve", memT[:, :, t * 128:(t + 1) * 128], pst[t][:],
                 gcol.unsqueeze(2).to_broadcast([128, KD, 128]), ALU.mult, [pst[t]], [memT])
        s.barrier()


def phase_cross(k, c, hT_dram, memT, w_xq_l, w_xkv_l, w_xo_l, x_src, x_dst, after_loads=None):
    s = k.s
    Stager._id = 0
    WStream._id = 0
    with contextlib.ExitStack() as st0:
        wq = s.sb([128, KD, D], BF16, "xwq", st0)
        wo = s.sb([128, KD, D], BF16, "xwo", st0)
        KT = s.sb([128, KD, MEM], BF16, "xKT", st0)
        Vm = s.sb([128, 2, D], BF16, "xVm", st0)
        with contextlib.ExitStack() as st:
            sg = Stager(k, KD, 256, st, cast_eng=("dve", "act", "pool"))
            wkv = [s.sb([128, KD, 256], BF16, "wkv", st) for _ in range(2)]
            pk = [s.ps([128, 512], F32, "xpk", st) for _ in range(2)]
            pi = 0
            for j in range(8):
                w = wkv[j % 2]
                sg.load(w_xkv_l[:, j * 256:(j + 1) * 256], w, w[:])
                if j < 4:
                    for mm_ in range(2):
                        ps = pk[pi % 2]
                        pi += 1
                        for kk in range(KD):
                            k.mm(ps[:, 0:MEM], w[:, kk, mm_ * 128:(mm_ + 1) * 128], memT[:, kk, :], kk == 0, kk == KD - 1, [w, memT], [ps])
                        k.copy(k.alt(), KT[:, j * 2 + mm_, :], ps[:, 0:MEM], [ps], [KT])
                else:
                    for t in range(2):
                        ps = pk[pi % 2]
                        pi += 1
                        for kk in range(KD):
                            k.mm(ps[:, 0:256], memT[:, kk, t * 128:(t + 1) * 128], w[:, kk, :], kk == 0, kk == KD - 1, [w, memT], [ps])
                        k.copy(k.alt(), Vm[:, t, (j - 4) * 256:(j - 3) * 256], ps[:, 0:256], [ps], [Vm])
            for j in range(4):
                sg.load(w_xq_l[:, j * 256:(j + 1) * 256], wq, wq[:, :, j * 256:(j + 1) * 256])
            for j in range(4):
                sg.load(w_xo_l[:, j * 256:(j + 1) * 256], wo, wo[:, :, j * 256:(j + 1) * 256])
            s.barrier()
        if after_loads is not None:
            after_loads()
        hc = [s.sb([128, KD, 512], BF16, "xhc", st0) for _ in range(2)]
        qT = [s.sb([128, KD, 512], BF16, "xqT", st0) for _ in range(2)]
        oT = [s.sb([128, KD, 512], BF16, "xoT", st0) for _ in range(2)]
        E = [s.sb([128, 2, 512], BF16, "xE", st0) for _ in range(2)]
        rden = [s.sb([128, 512], F32, "xrd", st0) for _ in range(2)]
        xt = [s.sb([128, D], F32, "xxt", st0) for _ in range(2)]
        psq = [s.ps([128, 512], F32, "xpsq", st0) for _ in range(2)]
        psS = [s.ps([128, 512], F32, "xpsS", st0) for _ in range(2)]
        psO = [s.ps([128, 512], F32, "xpsO", st0) for _ in range(2)]
        psD = s.ps([128, 512], F32, "xpsD", st0)
        psX = s.ps([128, 512], F32, "xpsX", st0)
        qi = 0
        ei = 0
        xi = 0
        xstate = {"xi": 0}

        def xo_tile(tc_, t):
            o_p = oT[tc_ % 2]
            T = tc_ * 4 + t
            xi_ = xstate["xi"]
            xstate["xi"] += 1
            x_t = xt[xi_ % 2]
            s.dma(x_t[:], x_src[T * 128:(T + 1) * 128, :], writes=[x_t], key=f"xx{xi_ % 2}")
            for hf in range(2):
                for kk in range(KD):
                    k.mm(psX[:], o_p[:, kk, t * 128:(t + 1) * 128], wo[:, kk, hf * 512:(hf + 1) * 512],
                         kk == 0, kk == KD - 1, [o_p, wo], [psX])
                k.tt("dve", x_t[:, hf * 512:(hf + 1) * 512], psX[:], x_t[:, hf * 512:(hf + 1) * 512], ALU.add, [psX, x_t], [x_t])
            s.dma(x_dst[T * 128:(T + 1) * 128, :], x_t[:], reads=[x_t], key=f"xxo{xi_ % 2}", eng="act")
        s.dma(hc[0][:], hT_dram[:, :, 0:512], writes=[hc[0]], key="xh0")
        for tc in range(NCH):
            h = hc[tc % 2]
            q_ = qT[tc % 2]
            o_ = oT[tc % 2]
            tk = slice(tc * 512, (tc + 1) * 512)
            if tc + 1 < NCH:
                s.dma(hc[(tc + 1) % 2][:], hT_dram[:, :, (tc + 1) * 512:(tc + 2) * 512], writes=[hc[(tc + 1) % 2]], key=f"xh{(tc + 1) % 2}")
            for m in range(KD):
                ps = psq[qi % 2]
                qi += 1
                for kk in range(KD):
                    k.mm(ps[:], wq[:, kk, m * 128:(m + 1) * 128], h[:, kk, :], kk == 0, kk == KD - 1, [wq, h], [ps])
                k.copy(k.alt(), q_[:, m, :], ps[:], [ps], [q_])
            for hd in range(4):
                if tc > 0:
                    xo_tile(tc - 1, hd)
                e_ = E[ei % 2]
                rd = rden[ei % 2]
                ei += 1
                for jt in range(2):
                    ps = psS[jt]
                    for hf in range(2):
                        k.mm(ps[:], KT[:, hd * 2 + hf, jt * 128:(jt + 1) * 128], q_[:, hd * 2 + hf, :], hf == 0, hf == 1, [KT, q_], [ps])
                    k.act(e_[:, jt, :], ps[:], AF.Exp, [ps], [e_], scale=1.0 / 16.0)
                for jt in range(2):
                    k.mm(psD[:], c["ones"][:], e_[:, jt, :], jt == 0, jt == 1, [c["ones"], e_], [psD])
                k.act(rd[:], psD[:], AF.Ln, [psD], [rd])
                k.act(rd[:], rd[:], AF.Exp, [rd], [rd], scale=-1.0)
                for cc in range(2):
                    ps = psO[cc]
                    for jt in range(2):
                        k.mm(ps[:], Vm[:, jt, hd * 256 + cc * 128: hd * 256 + (cc + 1) * 128], e_[:, jt, :], jt == 0, jt == 1, [Vm, e_], [ps])
                    k.tt("dve", o_[:, hd * 2 + cc, :], ps[:], rd[:], ALU.mult, [ps, rd], [o_])
        for t in range(4):
            xo_tile(NCH - 1, t)
        s.barrier()


def mlp_prefetch_up(k, st, w_up_l):
    s = k.s
    wu = s.sb([128, KD, 4 * D], BF16, "wu", st)
    ub = [[s.buf("wub") for _ in range(KD)] for _ in range(4)]

    def issue():
        for qd in range(4):
            for kk in range(KD):
                s.wload(wu[:, kk, qd * 1024:(qd + 1) * 1024], w_up_l[kk * 128:(kk + 1) * 128, qd * 1024:(qd + 1) * 1024],
                        [ub[qd][kk]], partial=False)
    return wu, ub, issue


def phase_mlp(k, c, hT_dram, w_up_l, w_down_l, x_src, x_dst, pre=None):
    s = k.s
    Stager._id = 0
    WStream._id = 0
    NF = 32
    with contextlib.ExitStack() as st0:
        if pre is None:
            wu, ub, issue = mlp_prefetch_up(k, st0, w_up_l)
            issue()
        else:
            wu, ub = pre
        wd = s.sb([128, NF, D], BF16, "wd", st0)
        db = [s.buf("wdb") for _ in range(NF)]
        for j in range(NF):
            s.wload(wd[:, j, :], w_down_l[j * 128:(j + 1) * 128, :], [db[j]], partial=False)
        hc = [s.sb([128, KD, 512], BF16, "ghc", st0) for _ in range(2)]
        rl = [s.sb([128, 512], BF16, "grl", st0) for _ in range(2)]
        aT = s.sb([128, NF, 512], BF16, "aT", st0)
        ab = [s.buf("aTb") for _ in range(NF)]
        xt = [s.sb([128, D], F32, "gxt", st0) for _ in range(2)]
        psU = [s.ps([128, 512], F32, "psU", st0) for _ in range(3)]
        psDn = [s.ps([128, 512], F32, "psDn", st0) for _ in range(2)]
        ui = 0
        xi = 0
        s.dma(hc[0][:], hT_dram[:, :, 0:512], writes=[hc[0]], key="gh0")
        for tc in range(NCH):
            h = hc[tc % 2]
            tk = slice(tc * 512, (tc + 1) * 512)
            if tc + 1 < NCH:
                s.dma(hc[(tc + 1) % 2][:], hT_dram[:, :, (tc + 1) * 512:(tc + 2) * 512], writes=[hc[(tc + 1) % 2]], key=f"gh{(tc + 1) % 2}")
            for f in range(NF):
                ps = psU[ui % 3]
                r_ = rl[ui % 2]
                ui += 1
                for kk in range(KD):
                    k.mm(ps[:], wu[:, kk, f * 128:(f + 1) * 128], h[:, kk, :], kk == 0, kk == KD - 1, [ub[f // 8][kk], h], [ps])
                k.act(r_[:], ps[:], AF.Relu, [ps], [r_])
                k.tt("pool", aT[:, f, :], r_[:], r_[:], ALU.mult, [r_], [ab[f]])
            for t in range(4):
                T = tc * 4 + t
                x_t = xt[xi % 2]
                xi += 1
                s.dma(x_t[:], x_src[T * 128:(T + 1) * 128, :], writes=[x_t], key=f"gx{xi % 2}")
                for hf in range(2):
                    ps = psDn[hf]
                    for f in range(NF):
                        k.mm(ps[:], aT[:, f, t * 128:(t + 1) * 128], wd[:, f, hf * 512:(hf + 1) * 512],
                             f == 0, f == NF - 1, [ab[f], db[f]], [ps])
                    k.tt("dve", x_t[:, hf * 512:(hf + 1) * 512], ps[:], x_t[:, hf * 512:(hf + 1) * 512], ALU.add, [ps, x_t], [x_t])
                s.dma(x_dst[T * 128:(T + 1) * 128, :], x_t[:], reads=[x_t], key=f"gxo{xi % 2}", eng="act")
        s.barrier()


def phase_final(k, c, x_src, gfull_dram, out_dram):
    s = k.s
    with contextlib.ExitStack() as st:
        gb = s.sb([128, D], F32, "gfin", st)
        s.dma(gb[:], gfull_dram, writes=[gb], key="gfin")
        xb = [s.sb([128, D], F32, "fxb", st) for _ in range(3)]
        junk = s.sb([128, D], BF16, "fjunk", st)
        yo = [s.sb([128, D], F32, "fyo", st) for _ in range(2)]
        ss = s.sb([128, NT], F32, "fss", st)
        ssb = [s.buf("fssb") for _ in range(NT)]
        k.memset("pool", ss[:], 0.0, ssb)
        outs = []

        def fa(t):
            x_t = xb[t % 3]
            s.dma(x_t[:], x_src[t * 128:(t + 1) * 128, :], writes=[x_t], key=f"fx{t % 3}")
            k.act(junk[:], x_t[:], AF.Square, [x_t], [junk, ssb[t]], accum_out=ss[:, t:t + 1])
            k.act(ss[:, t:t + 1], ss[:, t:t + 1], AF.Sqrt, [ssb[t]], [ssb[t]], bias=1e-6, scale=1.0 / D)
            s.op("dve", lambda e, t=t: e.reciprocal(out=ss[:, t:t + 1], in_=ss[:, t:t + 1]), [ssb[t]], [ssb[t]])

        def fb(t):
            x_t = xb[t % 3]
            y = yo[t % 2]
            k.stt("dve", y[:], x_t[:], ss[:, t:t + 1], gb[:], ALU.mult, ALU.mult, [x_t, ssb[t], gb], [y])
            outs.append(s.dma(out_dram[t * 128:(t + 1) * 128, :], y[:], reads=[y], key=f"fo{t % 2}", eng="act"))
        fa(0)
        fa(1)
        for t in range(NT):
            if t + 2 < NT:
                fa(t + 2)
            fb(t)
        s.barrier()
    return outs


def prep_s5(a_re, a_im, log_step, b_re, b_im, c_re, c_im, d_skip):
    f = np.float32
    bre = np.zeros((128, 16, 64), f)
    bim = np.zeros((128, 16, 64), f)
    ar = np.zeros((128, 16, 64), f)
    ai = np.zeros((128, 16, 64), f)
    ls = np.zeros((128, 16, 64), f)
    dm = np.zeros((128, 16, 64), f)
    for g in range(32):
        kc, hb, j = g // 8, (g % 8) // 4, g % 4
        idx = kc * 4 + j
        rows = slice(64 * hb + 16 * j, 64 * hb + 16 * j + 16)
        bre[rows, idx, :] = b_re[g].T
        bim[rows, idx, :] = b_im[g].T
        blk = slice(64 * hb, 64 * hb + 64)
        ar[blk, idx, :] = a_re[g][None, :]
        ai[blk, idx, :] = a_im[g][None, :]
        ls[blk, idx, :] = log_step[g]
        gp = g // 2
        off = 16 * (g % 2) + (32 if gp % 4 == 3 else 0)
        for cc in range(16):
            dm[64 * hb + 16 * j + cc, gp, off + cc] = d_skip[g * 16 + cc]
    st = np.zeros((128, 3, 32), f)
    st[0:64, 0, :] = a_re.T
    st[64:128, 0, :] = a_re.T
    st[0:64, 1, :] = a_im.T
    st[64:128, 1, :] = a_im.T
    st[:, 2, :] = log_step[None, :]
    c1 = np.zeros((128, 32, 16), f)
    c2 = np.zeros((128, 32, 16), f)
    c1[0:64] = c_re.transpose(2, 0, 1)
    c1[64:128] = c_im.transpose(2, 0, 1)
    c2[0:64] = c_im.transpose(2, 0, 1)
    c2[64:128] = c_re.transpose(2, 0, 1)
    return dict(bre=bre, bim=bim, ar=ar, ai=ai, ls=ls, st=st, c1=c1, c2=c2, dm=dm)


def col_layout(v):
    return np.ascontiguousarray(np.asarray(v, np.float32).reshape(-1, 128).T)


S5_SHAPES = [("bre", [128, 16, 64]), ("bim", [128, 16, 64]), ("ar", [128, 16, 64]), ("ai", [128, 16, 64]),
             ("ls", [128, 16, 64]), ("st", [128, 3, 32]), ("c1", [128, 32, 16]), ("c2", [128, 32, 16]), ("dm", [128, 16, 64])]
NVEC = 80


def build_program(n_layers=DEPTH, dbg=False):
    nc = bass.Bass("TRN2", target_bir_lowering=False)

    def din(name, shape, dt=F32):
        return nc.dram_tensor(name, list(shape), dt, kind="ExternalInput").ap()

    def dscr(name, shape, dt):
        if dbg:
            return nc.dram_tensor(name, list(shape), dt, kind="ExternalOutput").ap()
        return nc.dram_tensor(name, list(shape), dt).ap()

    x = din("x", [S, D])
    mem = din("mem", [MEM, D])
    vec = din("vec", [128, NVEC])
    gfin = din("gfin", [128, D])
    w_in = din("w_in", [DEPTH, D, D_IN])
    w_glu = din("w_glu", [DEPTH, 512, 512])
    w_gate = din("w_gla_gate", [DEPTH, 16, 256])
    b_gate = din("b_gla_gate", [DEPTH, 1, 256])
    w_branch = din("w_branch", [DEPTH, 3, 512, D])
    w_out = din("w_out", [DEPTH, D, D])
    w_xq = din("w_xq", [DEPTH, D, D])
    w_xkv = din("w_xkv", [DEPTH, D, 2 * D])
    w_xo = din("w_xo", [DEPTH, D, D])
    w_up = din("w_up", [DEPTH, D, 4 * D])
    w_down = din("w_down", [DEPTH, 4 * D, D])
    p5 = {nm: din("s5_" + nm, [DEPTH] + shp) for nm, shp in S5_SHAPES}
    out = nc.dram_tensor("out", [S, D], F32, kind="ExternalOutput").ap()
    hT = dscr("hT_scr", [128, KD, S], BF16)
    yaT = dscr("yaT_scr", [128, 4, S], BF16)
    ybT = dscr("ybT_scr", [128, 4, S], BF16)
    ycT = dscr("ycT_scr", [128, 4, S], BF16)
    xr = dscr("xr_scr", [S, D], F32)

    k = K(nc)
    s = k.s
    c = make_consts(k)
    vs = s.sb([128, NVEC], F32, "vec")
    s.dma(vs[:], vec, writes=[vs], key="vec")
    memT = s.sb([128, KD, MEM], BF16, "memT")
    s.barrier()
    phase_memnorm(k, c, mem, vs[:, 64:72], memT)
    x_cur = x
    for l in range(n_layers):
        o = 32 * l
        phase_norm(k, c, x_cur, vs[:, o:o + 8], hT)
        phase_s5(k, c, hT, w_in[l], {nm: p5[nm][l] for nm, _ in S5_SHAPES}, w_glu[l], vs[:, o + 24:o + 28], yaT)
        phase_gla(k, c, hT, w_in[l], w_gate[l], b_gate[l], vs[:, o + 28:o + 29], ybT)
        phase_attn(k, c, hT, w_in[l], ycT)
        phase_merge(k, c, hT, [yaT, ybT, ycT], w_in[l], w_branch[l], w_out[l], x_cur, xr)
        x_cur = xr
        phase_norm(k, c, xr, vs[:, o + 8:o + 16], hT)
        with contextlib.ExitStack() as lst:
            wu, ub, issue = mlp_prefetch_up(k, lst, w_up[l])
            phase_cross(k, c, hT, memT, w_xq[l], w_xkv[l], w_xo[l], xr, xr, after_loads=issue)
            phase_norm(k, c, xr, vs[:, o + 16:o + 24], hT)
            phase_mlp(k, c, hT, w_up[l], w_down[l], xr, xr, pre=(wu, ub))
    outs = phase_final(k, c, x_cur, gfin, out)
    s.emit(final_wait_ops=outs)
    info = {"ops": {e: len(v) for e, v in s.ops.items()}, "waits": s.nwaits,
            "maxcount": max(s.counters.values()), "nsem": len(s.counters)}
    s.close()
    return nc, info


def host_inputs(inp, b):
    f = np.float32
    vec = np.zeros((128, NVEC), f)
    for l in range(DEPTH):
        o = 32 * l
        vec[:, o:o + 8] = col_layout(inp["g_mix"][l])
        vec[:, o + 8:o + 16] = col_layout(inp["g_cross"][l])
        vec[:, o + 16:o + 24] = col_layout(inp["g_mlp"][l])
        vec[:, o + 24:o + 28] = col_layout(inp["b_glu"][l])
        vec[:, o + 28:o + 29] = col_layout(inp["g_gla_out"][l])
    vec[:, 64:72] = col_layout(inp["g_mem"])
    m = {
        "x": np.ascontiguousarray(inp["x"][b], dtype=f),
        "mem": np.ascontiguousarray(inp["mem"][b], dtype=f),
        "vec": vec,
        "gfin": np.ascontiguousarray(np.broadcast_to(np.asarray(inp["g_final"], f)[None, :], (128, D))),
        "b_gla_gate": np.ascontiguousarray(np.asarray(inp["b_gla_gate"], f)[:, None, :]),
    }
    for nm in ("w_in", "w_glu", "w_gla_gate", "w_branch", "w_out", "w_xq", "w_xkv", "w_xo", "w_up", "w_down"):
        m[nm] = np.ascontiguousarray(inp[nm], dtype=f)
    per = [prep_s5(inp["s5_a_re"][l], inp["s5_a_im"][l], inp["s5_log_step"][l], inp["s5_b_re"][l], inp["s5_b_im"][l],
                   inp["s5_c_re"][l], inp["s5_c_im"][l], inp["s5_d"][l]) for l in range(DEPTH)]
    for nm, _ in S5_SHAPES:
        m["s5_" + nm] = np.stack([per[l][nm] for l in range(DEPTH)], axis=0)
    return m


_CACHE = {}


def kernel(**inputs):
    inp = {k_: np.asarray(v) for k_, v in inputs.items()}
    if "nc" not in _CACHE:
        _CACHE["nc"] = build_program()[0]
    nc = _CACHE["nc"]
    n = inp["x"].shape[0]
    shared = host_inputs(inp, 0)
    in_maps = []
    for b in range(n):
        m = dict(shared)
        m["x"] = np.ascontiguousarray(inp["x"][b], dtype=np.float32)
        m["mem"] = np.ascontiguousarray(inp["mem"][b], dtype=np.float32)
        in_maps.append(m)
    res = run_bass_kernel_spmd(nc, in_maps, core_ids=list(range(n)))
    return np.stack([np.asarray(r["out"]) for r in res.results], axis=0).astype(np.float32)
```
